# Optimizing a Trainium2 kernel written in Bass

```python
import math
import jax, jax.numpy as jnp
from jax import lax
import numpy as np

D_MODEL = 1024
BATCH = 8
SEQ = 2048
DEPTH = 1
DEC_BATCH = 128
DEC_SEQ = 1
PAST_LEN = 16384
PAGE_SIZE = 128

MIX_WIDTH = D_MODEL
W_A = MIX_WIDTH // 2
W_B = MIX_WIDTH - W_A
S5_H = 16
S5_GROUPS = W_A // S5_H
S5_STATE = 64
POOL_WINDOWS = (2, 4, 8, 16)
POOL_GROUPS = len(POOL_WINDOWS)
POOL_CH = W_B // POOL_GROUPS
POOL_BUF = max(POOL_WINDOWS) - 1
D_FF = 4 * D_MODEL
DT_MIN = 0.001
DT_MAX = 0.1
EPS = 1e-6

kernel_name = "s5_pool_parallel_hybrid_step"


def rms_norm(x, g):
    xf = x.astype(jnp.float32)
    y = xf * lax.rsqrt(jnp.mean(xf * xf, axis=-1, keepdims=True) + EPS)
    return (y * g.astype(jnp.float32)).astype(x.dtype)


def _scan_combine(left, right):
    a_l, b_l = left
    a_r, b_r = right
    return a_r * a_l, a_r * b_l + b_r


def s5_mixer(u, h0_re, h0_im, lam_re, lam_im, log_dt, b_re, b_im, c_re, c_im, d, w_glu):
    f32 = jnp.float32
    bsz, L, _ = u.shape
    lam = lax.complex(lam_re.astype(f32), lam_im.astype(f32))
    dt = jnp.exp(log_dt.astype(f32))[:, None]
    lam_bar = jnp.exp(lam * dt)
    b_mat = lax.complex(b_re.astype(f32), b_im.astype(f32))
    b_bar = ((lam_bar - 1.0) / lam)[..., None] * b_mat
    uf = u.astype(f32).reshape(bsz, L, S5_GROUPS, S5_H)
    bu = jnp.einsum('gph,blgh->blgp', b_bar, uf.astype(jnp.complex64))
    h0 = lax.complex(h0_re.astype(f32), h0_im.astype(f32))
    bu = bu.at[:, 0].add(lam_bar[None] * h0)
    a = jnp.broadcast_to(lam_bar, bu.shape)
    _, hs = lax.associative_scan(_scan_combine, (a, bu), axis=1)
    c_mat = lax.complex(c_re.astype(f32), c_im.astype(f32))
    y = jnp.real(jnp.einsum('ghp,blgp->blgh', c_mat, hs))
    y = y + d.astype(f32).reshape(S5_GROUPS, S5_H) * uf
    y = jax.nn.gelu(y)
    y = y * jax.nn.sigmoid(jnp.einsum('blgh,ghk->blgk', y, w_glu.astype(f32)))
    h_last = hs[:, -1]
    return (y.reshape(bsz, L, W_A).astype(u.dtype),
            jnp.real(h_last).astype(h0_re.dtype),
            jnp.imag(h_last).astype(h0_im.dtype))


def pool_mixer(u, prefix, start_pos, pool_w, pool_scale):
    f32 = jnp.float32
    bsz, L, _ = u.shape
    u_ext = jnp.concatenate([prefix.astype(u.dtype), u], axis=1)
    uf = u.astype(f32)
    cs = jnp.cumsum(u_ext.astype(f32), axis=1)
    cs = jnp.concatenate([jnp.zeros((bsz, 1, W_B), f32), cs], axis=1)
    pos = start_pos + jnp.arange(L)
    off = POOL_BUF + 1
    outs = []
    for gi, w in enumerate(POOL_WINDOWS):
        sl = slice(gi * POOL_CH, (gi + 1) * POOL_CH)
        win = cs[:, off:off + L, sl] - cs[:, off - w:off - w + L, sl]
        count = jnp.minimum(pos + 1, w).astype(f32)[None, :, None]
        pooled = win / count - uf[..., sl]
        outs.append(jnp.einsum('blc,cd->bld', pooled, pool_w[gi].astype(f32)))
    y = jnp.concatenate(outs, axis=-1) * pool_scale.astype(f32)
    new_buf = u_ext[:, -POOL_BUF:]
    return y.astype(u.dtype), new_buf.astype(prefix.dtype)


def layer(x, h0_re, h0_im, pool_prefix, start_pos,
          norm_mix_pre, norm_mix_post, norm_mlp_pre, norm_mlp_post,
          w_in, s5_lambda_re, s5_lambda_im, s5_log_dt, s5_b_re, s5_b_im,
          s5_c_re, s5_c_im, s5_d, s5_w_glu, pool_w, pool_scale, w_out,
          w_mlp_up, w_mlp_down):
    xn = rms_norm(x, norm_mix_pre)
    u = jnp.einsum('bld,dm->blm', xn, w_in)
    y_a, h_re, h_im = s5_mixer(u[..., :W_A], h0_re, h0_im, s5_lambda_re, s5_lambda_im,
                               s5_log_dt, s5_b_re, s5_b_im, s5_c_re, s5_c_im, s5_d, s5_w_glu)
    y_b, buf = pool_mixer(u[..., W_A:], pool_prefix, start_pos, pool_w, pool_scale)
    mix = jnp.einsum('blm,md->bld', jnp.concatenate([y_a, y_b], axis=-1), w_out)
    h = x + rms_norm(mix, norm_mix_post)
    hn = rms_norm(h, norm_mlp_pre)
    ff = jnp.square(jax.nn.relu(jnp.einsum('bld,df->blf', hn, w_mlp_up)))
    ff = jnp.einsum('blf,fd->bld', ff, w_mlp_down)
    out = h + rms_norm(ff, norm_mlp_post)
    return out, h_re, h_im, buf


def setup_inputs(seed: int = 0) -> dict:
    key = jax.random.key(seed)
    ks = jax.random.split(key, 24)
    f32 = jnp.float32
    n = jnp.arange(S5_STATE, dtype=f32)
    lam_re = -0.5 + 0.01 * jax.random.normal(ks[5], (S5_GROUPS, S5_STATE), f32)
    lam_im = math.pi * n[None, :] + 0.01 * jax.random.normal(ks[6], (S5_GROUPS, S5_STATE), f32)
    log_dt = jax.random.uniform(ks[7], (S5_GROUPS,), f32, math.log(DT_MIN), math.log(DT_MAX))
    b_scale = (2.0 * S5_H) ** -0.5
    c_scale = (2.0 * S5_STATE) ** -0.5
    return {
        "x_prompt": jax.random.normal(ks[0], (BATCH, SEQ, D_MODEL), f32),
        "x_sample": jax.random.normal(ks[1], (DEC_BATCH, DEC_SEQ, D_MODEL), f32),
        "state_s5_re": 0.5 * jax.random.normal(ks[2], (DEC_BATCH, S5_GROUPS, S5_STATE), f32),
        "state_s5_im": 0.5 * jax.random.normal(ks[3], (DEC_BATCH, S5_GROUPS, S5_STATE), f32),
        "state_pool": jax.random.normal(ks[4], (DEC_BATCH, POOL_BUF, W_B), f32),
        "norm_mix_pre": 1.0 + 0.05 * jax.random.normal(ks[8], (D_MODEL,), f32),
        "norm_mix_post": 1.0 + 0.05 * jax.random.normal(ks[9], (D_MODEL,), f32),
        "norm_mlp_pre": 1.0 + 0.05 * jax.random.normal(ks[10], (D_MODEL,), f32),
        "norm_mlp_post": 1.0 + 0.05 * jax.random.normal(ks[11], (D_MODEL,), f32),
        "w_in": jax.random.normal(ks[12], (D_MODEL, MIX_WIDTH), f32) * D_MODEL ** -0.5,
        "s5_lambda_re": lam_re,
        "s5_lambda_im": lam_im,
        "s5_log_dt": log_dt,
        "s5_b_re": jax.random.normal(ks[13], (S5_GROUPS, S5_STATE, S5_H), f32) * b_scale,
        "s5_b_im": jax.random.normal(ks[14], (S5_GROUPS, S5_STATE, S5_H), f32) * b_scale,
        "s5_c_re": jax.random.normal(ks[15], (S5_GROUPS, S5_H, S5_STATE), f32) * c_scale,
        "s5_c_im": jax.random.normal(ks[16], (S5_GROUPS, S5_H, S5_STATE), f32) * c_scale,
        "s5_d": jax.random.normal(ks[17], (W_A,), f32),
        "s5_w_glu": jax.random.normal(ks[18], (S5_GROUPS, S5_H, S5_H), f32) * S5_H ** -0.5,
        "pool_w": jax.random.normal(ks[19], (POOL_GROUPS, POOL_CH, POOL_CH), f32) * POOL_CH ** -0.5,
        "pool_scale": 1.0 + 0.1 * jax.random.normal(ks[20], (W_B,), f32),
        "w_out": jax.random.normal(ks[21], (MIX_WIDTH, D_MODEL), f32) * MIX_WIDTH ** -0.5,
        "w_mlp_up": jax.random.normal(ks[22], (D_MODEL, D_FF), f32) * D_MODEL ** -0.5,
        "w_mlp_down": jax.random.normal(ks[23], (D_FF, D_MODEL), f32) * D_FF ** -0.5,
    }


def reference(x_prompt, x_sample, state_s5_re, state_s5_im, state_pool,
              norm_mix_pre, norm_mix_post, norm_mlp_pre, norm_mlp_post,
              w_in, s5_lambda_re, s5_lambda_im, s5_log_dt, s5_b_re, s5_b_im,
              s5_c_re, s5_c_im, s5_d, s5_w_glu, pool_w, pool_scale, w_out,
              w_mlp_up, w_mlp_down):
    weights = (norm_mix_pre, norm_mix_post, norm_mlp_pre, norm_mlp_post,
               w_in, s5_lambda_re, s5_lambda_im, s5_log_dt, s5_b_re, s5_b_im,
               s5_c_re, s5_c_im, s5_d, s5_w_glu, pool_w, pool_scale, w_out,
               w_mlp_up, w_mlp_down)
    hp_re = jnp.zeros((x_prompt.shape[0], S5_GROUPS, S5_STATE), state_s5_re.dtype)
    hp_im = jnp.zeros((x_prompt.shape[0], S5_GROUPS, S5_STATE), state_s5_im.dtype)
    pp = jnp.zeros((x_prompt.shape[0], POOL_BUF, W_B), state_pool.dtype)
    hs_re, hs_im, ps = state_s5_re, state_s5_im, state_pool
    y_p, y_s = x_prompt, x_sample
    for _ in range(DEPTH):
        y_p, hp_re, hp_im, pp = layer(y_p, hp_re, hp_im, pp, 0, *weights)
        y_s, hs_re, hs_im, ps = layer(y_s, hs_re, hs_im, ps, PAST_LEN, *weights)
    return (y_p, y_s, hp_re, hp_im, pp, hs_re, hs_im, ps)
```

```python
import math
from contextlib import ExitStack

import numpy as np
import concourse.bass as bass
import concourse.mybir as mybir
from concourse.bass_utils import run_bass_kernel_spmd

F32 = mybir.dt.float32
BF16 = mybir.dt.bfloat16
AF = mybir.ActivationFunctionType
ALU = mybir.AluOpType
AX = mybir.AxisListType

D = 1024
SEQ = 2048
NSMP = 16
NT = SEQ + NSMP
DFF = 4096
EPS = 1e-6
POOL_W = (2, 4, 8, 16)
NCORES = 8


class Tok:
    __slots__ = ("sem", "val", "eng")

    def __init__(self, sem, val, eng):
        self.sem, self.val, self.eng = sem, val, eng


class KB:
    def __init__(self, nc, es):
        self.nc = nc
        self.es = es
        self.eng = {"pe": nc.tensor, "act": nc.scalar, "dve": nc.vector, "pool": nc.gpsimd, "sp": nc.sync}
        self.sem = {}
        self.cnt = {}
        for e in ("pe", "act", "dve", "pool"):
            self.sem[e] = es.enter_context(nc.semaphore("sem_" + e))
            self.cnt[e] = 0
        self.pending = {e: [] for e in ("pe", "act", "dve", "pool")}
        self.waited = {}
        self.buf = {}
        self.dsems = {}
        self.dcnt = {}
        self.out_tokens = []
        self.halted = False

    def _deps(self, reads, writes):
        deps = []
        for k in reads:
            st = self.buf.get(k)
            if st and st["w"] is not None:
                deps.append(st["w"])
        for k in writes:
            st = self.buf.get(k)
            if st:
                if st["w"] is not None:
                    deps.append(st["w"])
                deps.extend(st["r"])
        return deps

    def _update(self, tok, reads, writes):
        for k in reads:
            st = self.buf.setdefault(k, {"w": None, "r": []})
            st["r"].append(tok)
            if len(st["r"]) > 64:
                st["r"] = self._prune(st["r"])
        for k in writes:
            self.buf[k] = {"w": tok, "r": []}

    @staticmethod
    def _prune(toks):
        best = {}
        for t in toks:
            key = id(t.sem)
            if t.val is None or key not in best or (best[key].val is not None and t.val > best[key].val):
                if t.val is None:
                    best[(key, id(t))] = t
                else:
                    best[key] = t
        return list(best.values())

    def _wait(self, e, deps):
        if self.halted:
            return
        eng = self.eng[e]
        need = {}
        for t in deps:
            if t.eng == "pe" and e == "pe":
                continue
            if t.val is None:
                raise RuntimeError("dependency on an unsignalled instruction (engine %s)" % t.eng)
            key = (e, id(t.sem))
            if self.waited.get(key, 0) >= t.val:
                continue
            if key not in need or need[key].val < t.val:
                need[key] = t
        for key, t in need.items():
            eng.wait_ge(t.sem, t.val)
            self.waited[key] = t.val

    def op(self, e, fn, reads=(), writes=(), sig=True, deps=()):
        if self.halted:
            return Tok(self.sem[e], 0, e)
        writes = list(writes) + [k for k in reads if k.startswith("pb") or k.startswith("ptb")]
        d = self._deps(reads, writes) + list(deps)
        self._wait(e, d)
        inst = fn(self.eng[e])
        if sig or e != "pe":
            self.cnt[e] += 1
            inst.then_inc(self.sem[e], 1)
            tok = Tok(self.sem[e], self.cnt[e], e)
            for p in self.pending[e]:
                p.val = self.cnt[e]
            self.pending[e] = []
        else:
            tok = Tok(self.sem[e], None, e)
            self.pending[e].append(tok)
        self._update(tok, reads, writes)
        return tok

    def dma(self, q, out, in_, reads=(), writes=(), sem=None, is_out=False, **kw):
        if self.halted:
            return Tok(self.sem["pe"], 0, "dma")
        d = self._deps(reads, writes)
        self._wait(q, d)
        name = (writes[0] if writes else ("o_" + reads[0] if reads else "out"))
        if name not in self.dsems:
            self.dsems[name] = self.es.enter_context(self.nc.semaphore("dsem_" + name))
            self.dcnt[name] = 0
        self.dcnt[name] += 16
        self.eng[q].dma_start(out=out, in_=in_, **kw).then_inc(self.dsems[name], 16)
        tok = Tok(self.dsems[name], self.dcnt[name], "dma")
        self._update(tok, reads, writes)
        if is_out:
            self.out_tokens.append(tok)
        return tok

    def finish(self):
        self._wait("sp", self.out_tokens)


class _Stop(Exception):
    pass


def build_program(debug=None, stop=None):
    nc = bass.Bass("TRN2", target_bir_lowering=False)
    dt_in = lambda name, shape: nc.dram_tensor(name, shape, F32, kind="ExternalInput").ap()
    dt_out = lambda name, shape: nc.dram_tensor(name, shape, F32, kind="ExternalOutput").ap()

    xp = dt_in("xp", [SEQ, D])
    xs = dt_in("xs", [NSMP, D])
    st_re = dt_in("st_re", [NSMP, 2048])
    st_im = dt_in("st_im", [NSMP, 2048])
    st_pool = dt_in("st_pool", [NSMP, 15, 512])
    g_mix_pre = dt_in("g_mix_pre", [D])
    g_mix_post = dt_in("g_mix_post", [D])
    g_mlp_pre = dt_in("g_mlp_pre", [D])
    g_mlp_post = dt_in("g_mlp_post", [D])
    w_in = dt_in("w_in", [D, D])
    lam_re = dt_in("lam_re", [2048])
    lam_im = dt_in("lam_im", [2048])
    log_dt = dt_in("log_dt", [32])
    b_re = dt_in("b_re", [2048, 16])
    b_im = dt_in("b_im", [2048, 16])
    c_re = dt_in("c_re", [512, 64])
    c_im = dt_in("c_im", [512, 64])
    s5_d = dt_in("s5_d", [512])
    w_glu = dt_in("w_glu", [32, 16, 16])
    pool_w = dt_in("pool_w", [4, 128, 128])
    pool_scale = dt_in("pool_scale", [512])
    w_out = dt_in("w_out", [D, D])
    w_up = dt_in("w_up", [D, DFF])
    w_dn = dt_in("w_dn", [DFF, D])

    y_p = dt_out("y_p", [SEQ, D])
    y_s = dt_out("y_s", [NSMP, D])
    o_re_p = dt_out("o_re_p", [2048])
    o_im_p = dt_out("o_im_p", [2048])
    o_pool_p = dt_out("o_pool_p", [15, 512])
    o_re_s = dt_out("o_re_s", [NSMP, 2048])
    o_im_s = dt_out("o_im_s", [NSMP, 2048])
    o_pool_s = dt_out("o_pool_s", [NSMP, 15, 512])
    dbg = None
    if debug:
        dbg = {k: nc.dram_tensor("dbg_" + k, list(shp), BF16, kind="ExternalOutput").ap() for k, shp in debug.items()}

    with ExitStack() as es:
      kb = KB(nc, es)
      try:
        op, dma = kb.op, kb.dma
        nm = [0]

        def sb(shape, dt, name=None, scope=es):
            nm[0] += 1
            return scope.enter_context(nc.sbuf_tensor(name or ("t%d" % nm[0]), list(shape), dt))

        def ps(shape, dt, name=None, scope=es):
            nm[0] += 1
            return scope.enter_context(nc.psum_tensor(name or ("p%d" % nm[0]), list(shape), dt))

        es.enter_context(nc.allow_non_contiguous_dma(reason="small strided parameter / state transfers"))

        ycat = sb([128, 8, NT], BF16, "ycat")
        identb = sb([128, 128], BF16, "identb")
        cst = sb([128, 8], F32, "cst")
        stat = sb([128, 64], F32, "stat")
        junk = sb([128, 512], BF16, "junk")
        pb = [ps([128, 512], F32, "pb%d" % i) for i in range(8)]
        ptb = [pb[6][:].bitcast(BF16), pb[7][:].bitcast(BF16)]

        p1 = es.enter_context(ExitStack())
        ident32 = sb([128, 128], F32, "ident32", p1)
        op("pool", lambda e: e.memset(ident32[:], 1.0), writes=["ident32"])
        op("pool", lambda e: e.affine_select(out=ident32[:], in_=ident32[:], pattern=[[-1, 128]],
                                             compare_op=ALU.is_equal, fill=0.0, base=0, channel_multiplier=1),
           reads=["ident32"], writes=["ident32"])
        op("dve", lambda e: e.tensor_copy(identb[:], ident32[:]), reads=["ident32"], writes=["identb"])
        for i, v in enumerate([EPS, -0.5, math.pi, 2 * math.pi, -math.pi, 1.5 * math.pi]):
            op("pool", lambda e, i=i, v=v: e.memset(cst[:, i:i + 1], v), writes=["cst"])

        def barrier():
            toks = [op(e_, lambda e, c_=c_: e.memset(cst[:, c_:c_ + 1], 0.0), writes=["bar_" + e_]) for e_, c_ in (("dve", 6), ("pool", 7))]
            toks.append(op("act", lambda e: e.activation(out=junk[:, 0:1], in_=cst[:, 0:1], func=AF.Copy), reads=["cst"], writes=["junk"]))
            toks.append(op("pe", lambda e: e.matmul(pb[5][0:16, 0:16], identb[:, 0:16], identb[:, 0:16], start=True, stop=True),
                           reads=["identb"], writes=["pb5"]))
            if kb.halted:
                return []
            toks += [Tok(kb.dsems[k_], kb.dcnt[k_], "dma") for k_ in kb.dsems]
            for e_ in ("dve", "pool", "act", "pe", "sp"):
                kb._wait(e_, toks)
            return toks

        stat_i = [0]

        def stat_col():
            stat_i[0] = (stat_i[0] + 1) % 60
            return stat_i[0]

        def rstd_from(srcs, rows, key_reads):
            cols = []
            srcs2 = []
            for ap_, key in srcs:
                n_ = ap_.shape[-1] if len(ap_.shape) == 2 else 1
                if n_ > 512:
                    for o_ in range(0, n_, 512):
                        srcs2.append((ap_[:, o_:o_ + 512], key))
                else:
                    srcs2.append((ap_, key))
            assert len(srcs2) <= 2
            for ap_, key in srcs2:
                c = stat_col()
                n = 1
                for v_ in ap_.shape[1:]:
                    n *= v_
                jv = junk[0:rows, 0:n]
                if len(ap_.shape) == 3:
                    jv = jv.rearrange("p (a b) -> p a b", b=ap_.shape[2])
                op("act", lambda e, ap_=ap_, c=c, n=n, jv=jv: e.activation(out=jv, in_=ap_, func=AF.Square,
                                                                 scale=1.0 / 32.0, accum_out=stat[0:rows, c:c + 1]),
                   reads=[key], writes=["junk", "stat%d" % c])
                cols.append(c)
            c2 = stat_col()
            if len(cols) == 2:
                op("dve", lambda e: e.scalar_tensor_tensor(out=stat[0:rows, c2:c2 + 1], in0=stat[0:rows, cols[0]:cols[0] + 1],
                                                           scalar=EPS, in1=stat[0:rows, cols[1]:cols[1] + 1],
                                                           op0=ALU.add, op1=ALU.add),
                   reads=["stat%d" % cols[0], "stat%d" % cols[1]], writes=["stat%d" % c2])
            else:
                op("dve", lambda e: e.tensor_scalar_add(out=stat[0:rows, c2:c2 + 1], in0=stat[0:rows, cols[0]:cols[0] + 1],
                                                        scalar1=EPS),
                   reads=["stat%d" % cols[0]], writes=["stat%d" % c2])
            c3 = stat_col()
            op("pool", lambda e: e.tensor_tensor(out=stat[0:rows, c3:c3 + 1], in0=stat[0:rows, c2:c2 + 1],
                                                 in1=cst[0:rows, 1:2], op=ALU.pow),
               reads=["stat%d" % c2, "cst"], writes=["stat%d" % c3])
            return stat[0:rows, c3:c3 + 1], "stat%d" % c3

        if True:
            w_in_sb = sb([128, 8, D], BF16, "w_in_sb", p1)
            g0 = sb([128, D], F32, "g0", p1)
            dma("sp", g0[:], g_mix_pre.partition_broadcast(128), writes=["g0"], sem="setup")

            toep_sb = sb([128, 32, 128], BF16, "toep_sb", p1)
            bpow_sb = sb([128, 32, 2, 64], BF16, "bpow_sb", p1)
            cpow_sb = sb([128, 16, 2, 128], BF16, "cpow_sb", p1)
            glu_sb = sb([128, 4, 128], BF16, "glu_sb", p1)
            poolw_sb = sb([128, 4, 128], BF16, "poolw_sb", p1)
            pscale = sb([128, 4], F32, "pscale", p1)
            invcnt = sb([128, 4, 16], F32, "invcnt", p1)
            ER = sb([128, 16, 128], F32, "ER", p1)
            EI = sb([128, 16, 128], F32, "EI", p1)
            rho8 = sb([128, 16], F32, "rho8", p1)
            P1r = sb([128, 16], F32, "P1r", p1)
            P1i = sb([128, 16], F32, "P1i", p1)
            carry = sb([128, 2, 16], F32, "carry", p1)

            with ExitStack() as su:
                lamr = sb([128, 16], F32, "lamr", su)
                lami = sb([128, 16], F32, "lami", su)
                ldt = sb([128, 16], F32, "ldt", su)
                Bq = [sb([128, 16, 16], F32, "Bq%d" % i, su) for i in range(2)]
                Cnat = [sb([128, 4, 64], F32, "Cnat%d" % i, su) for i in range(2)]
                Cq = [sb([128, 16, 16], F32, "Cq%d" % i, su) for i in range(2)]
                Dcol = sb([128, 32], F32, "Dcol", su)
                lam_nat = [sb([16, 128], F32, "lam_nat%d" % i, su) for i in range(2)]
                ldt_row = sb([1, 32], F32, "ldt_row", su)
                ones_row = sb([1, 128], F32, "ones_row", su)
                Dg = sb([32, 16], F32, "Dg", su)
                Dg8 = sb([32, 8, 16], F32, "Dg8", su)
                Wgl = sb([128, 4, 16], F32, "Wgl", su)
                psc_nat = sb([4, 128], F32, "psc_nat", su)
                bmask = sb([128, 8, 16], F32, "bmask", su)
                su_b = ExitStack()
                Bnat = [sb([16, 2048], F32, "Bnat%d" % i, su_b) for i in range(2)]
                dma("sp", lam_nat[0][:], lam_re.rearrange("(j q) -> j q", q=128), writes=["lam_nat0"])
                dma("sp", lam_nat[1][:], lam_im.rearrange("(j q) -> j q", q=128), writes=["lam_nat1"])
                dma("sp", ldt_row[:], log_dt.rearrange("(o g) -> o g", o=1), writes=["ldt_row"])
                dma("sp", Bnat[0][:], b_re.rearrange("(j q) h -> j (q h)", q=128), writes=["Bnat0"])
                dma("sp", Bnat[1][:], b_im.rearrange("(j q) h -> j (q h)", q=128), writes=["Bnat1"])
                dma("sp", Cnat[0][:], c_re.rearrange("(T r) p -> r T p", r=128), writes=["Cnat0"])
                dma("sp", Cnat[1][:], c_im.rearrange("(T r) p -> r T p", r=128), writes=["Cnat1"])
                dma("sp", Dg[:], s5_d.rearrange("(g h) -> g h", h=16), writes=["Dg"])
                dma("sp", Wgl[:], w_glu.rearrange("(ct g8) h k -> (g8 h) ct k", g8=8), writes=["Wgl"])
                dma("sp", psc_nat[:], pool_scale.rearrange("(g d) -> g d", d=128), writes=["psc_nat"])
                op("pool", lambda e: e.memset(ones_row[:], 1.0), writes=["ones_row"])
                for part in range(2):
                    op("pe", lambda e, part=part: e.matmul(pb[0][:, 16 * part:16 * part + 16], lam_nat[part][:], ident32[0:16, 0:16], start=True, stop=True),
                       reads=["lam_nat%d" % part, "ident32"], writes=["pb0"], sig=(part == 1))
                op("pe", lambda e: e.matmul(pb[0][:, 32:64], ones_row[:], ldt_row[:], start=True, stop=True),
                   reads=["ones_row", "ldt_row"], writes=["pb0"])
                op("pe", lambda e: e.matmul(pb[0][:, 64:68], psc_nat[:], ident32[0:4, 0:4], start=True, stop=True),
                   reads=["psc_nat", "ident32"], writes=["pb0"])
                op("dve", lambda e: e.tensor_copy(Dg8[:], Dg[:].unsqueeze(1).to_broadcast([32, 8, 16])), reads=["Dg"], writes=["Dg8"])
                op("pe", lambda e: e.matmul(pb[0][:, 96:128], Dg8[:].rearrange("p s h -> p (s h)"), ident32[0:32, 0:32], start=True, stop=True),
                   reads=["Dg8", "ident32"], writes=["pb0"])
                op("dve", lambda e: e.tensor_copy(lamr[:], pb[0][:, 0:16]), reads=["pb0"], writes=["lamr"])
                op("dve", lambda e: e.tensor_copy(lami[:], pb[0][:, 16:32]), reads=["pb0"], writes=["lami"])
                op("dve", lambda e: e.tensor_copy(ldt[0:64, :], pb[0][0:64, 32:64:2]), reads=["pb0"], writes=["ldt"])
                op("dve", lambda e: e.tensor_copy(ldt[64:128, :], pb[0][64:128, 33:64:2]), reads=["pb0"], writes=["ldt"])
                op("dve", lambda e: e.tensor_copy(pscale[:], pb[0][:, 64:68]), reads=["pb0"], writes=["pscale"])
                op("dve", lambda e: e.tensor_copy(Dcol[:], pb[0][:, 96:128]), reads=["pb0"], writes=["Dcol"])
                for part in range(2):
                    for h_ in range(16):
                        op("pe", lambda e, part=part, h_=h_: e.matmul(pb[1][:, (part * 16 + h_) * 16:(part * 16 + h_ + 1) * 16],
                                                                        Bnat[part][:, h_:2048:16], ident32[0:16, 0:16], start=True, stop=True),
                           reads=["Bnat%d" % part, "ident32"], writes=["pb1"], sig=(h_ == 15))
                    bq_tok = op("dve", lambda e, part=part: e.tensor_copy(Bq[part][:].rearrange("p j h -> p h j"),
                                                                 pb[1][:, part * 256:(part + 1) * 256].rearrange("p (h j) -> p h j", j=16)),
                       reads=["pb1"], writes=["Bq%d" % part])
                op("pool", lambda e: e.memset(bmask[:], 1.0), writes=["bmask"])
                op("pool", lambda e: e.affine_select(out=bmask[:], in_=bmask[:], pattern=[[16, 8], [0, 16]], compare_op=ALU.is_ge, fill=0.0,
                                                     base=15, channel_multiplier=-1), reads=["bmask"], writes=["bmask"])
                op("pool", lambda e: e.affine_select(out=bmask[:], in_=bmask[:], pattern=[[-16, 8], [0, 16]], compare_op=ALU.is_ge, fill=0.0,
                                                     base=0, channel_multiplier=1), reads=["bmask"], writes=["bmask"])
                op("dve", lambda e: e.tensor_tensor(out=glu_sb[:].rearrange("p c (g k) -> p c g k", k=16),
                                                    in0=Wgl[:].unsqueeze(2).to_broadcast([128, 4, 8, 16]),
                                                    in1=bmask[:].unsqueeze(1).to_broadcast([128, 4, 8, 16]), op=ALU.mult),
                   reads=["Wgl", "bmask"], writes=["glu_sb"])
                su_b.close()
                for e_ in ("pool", "act", "pe", "sp"):
                    kb._wait(e_, [bq_tok])
                for gi, w in enumerate(POOL_W):
                    for t in range(16):
                        op("pool", lambda e, gi=gi, t=t, w=w: e.memset(invcnt[:, gi, t:t + 1], 1.0 / min(t + 1, w)),
                           writes=["invcnt"])

                dtq = sb([128, 16], F32, "dtq", su)
                aq = sb([128, 16], F32, "aq", su)
                thq = sb([128, 16], F32, "thq", su)
                tmpq = [sb([128, 16], F32, "tmpq%d" % i, su) for i in range(6)]
                twopi = sb([128, 16], F32, "twopi", su)
                op("pool", lambda e: e.memset(twopi[:], 2 * math.pi), writes=["twopi"])
                def series(out_t, ok, x_t, xk, n, xscale, tmp_t, tk):
                    op("dve", lambda e: e.memset(out_t[:], 1.0), writes=[ok])
                    for k in range(n, 0, -1):
                        op("dve", lambda e, k=k: e.scalar_tensor_tensor(out=tmp_t[:], in0=out_t[:], scalar=xscale / k, in1=x_t[:],
                                                                        op0=ALU.mult, op1=ALU.mult), reads=[ok, xk], writes=[tk])
                        op("dve", lambda e: e.tensor_scalar_add(out=out_t[:], in0=tmp_t[:], scalar1=1.0), reads=[tk], writes=[ok])
                dma("pool", w_in_sb[:, 0:4, :], w_in.rearrange("(kt p) n -> p kt n", p=128)[:, 0:4, :], writes=["w_in_sb"], sem="w_in")
                dma("pool", w_in_sb[:, 4:8, :], w_in.rearrange("(kt p) n -> p kt n", p=128)[:, 4:8, :], writes=["w_in_sb"], sem="w_in")
                dma("pool", poolw_sb[:], pool_w.rearrange("g c d -> c g d"), writes=["poolw_sb"])
                series(dtq, "dtq", ldt, "ldt", 12, 0.125, tmpq[0], "tmpq0")
                for _ in range(3):
                    op("dve", lambda e: e.tensor_mul(dtq[:], dtq[:], dtq[:]), reads=["dtq"], writes=["dtq"])
                op("dve", lambda e: e.tensor_mul(aq[:], lamr[:], dtq[:]), reads=["lamr", "dtq"], writes=["aq"])
                op("dve", lambda e: e.tensor_mul(thq[:], lami[:], dtq[:]), reads=["lami", "dtq"], writes=["thq"])
                PR = sb([128, 9, 16], F32, "PR", su)
                PI = sb([128, 9, 16], F32, "PI", su)
                VR = sb([128, 9, 16], F32, "VR", su)
                VI = sb([128, 9, 16], F32, "VI", su)
                mag = sb([128, 16], F32, "mag", su)
                imag = sb([128, 16], F32, "imag", su)
                em1 = sb([128, 16], F32, "em1", su)
                cs = sb([128, 2, 16], F32, "cs", su)
                series(mag, "mag", aq, "aq", 7, 1.0, tmpq[0], "tmpq0")
                op("dve", lambda e: e.memset(em1[:], 1.0), writes=["em1"])
                for k in range(8, 1, -1):
                    op("dve", lambda e, k=k: e.scalar_tensor_tensor(out=tmpq[0][:], in0=em1[:], scalar=1.0 / k, in1=aq[:],
                                                                    op0=ALU.mult, op1=ALU.mult), reads=["em1", "aq"], writes=["tmpq0"])
                    op("dve", lambda e: e.tensor_scalar_add(out=em1[:], in0=tmpq[0][:], scalar1=1.0), reads=["tmpq0"], writes=["em1"])
                op("dve", lambda e: e.tensor_mul(em1[:], em1[:], aq[:]), reads=["em1", "aq"], writes=["em1"])
                m8 = tmpq[5]
                op("dve", lambda e: e.tensor_mul(imag[:], mag[:], mag[:]), reads=["mag"], writes=["imag"])
                op("dve", lambda e: e.tensor_mul(rho8[:], imag[:], imag[:]), reads=["imag"], writes=["rho8"])
                op("dve", lambda e: e.tensor_mul(rho8[:], rho8[:], rho8[:]), reads=["rho8"], writes=["rho8"])
                op("dve", lambda e: e.reciprocal(imag[:], imag[:]), reads=["imag"], writes=["imag"])
                op("dve", lambda e: e.reciprocal(m8[:], rho8[:]), reads=["rho8"], writes=["tmpq5"])
                qi = sb([128, 16], mybir.dt.int32, "qi", su)
                for idx, shift in ((0, 0.5 * math.pi), (1, 0.0)):
                    t0 = tmpq[idx]
                    t1_ = tmpq[2 + idx]
                    k0 = "tmpq%d" % idx
                    k1 = "tmpq%d" % (2 + idx)
                    op("dve", lambda e, t0=t0, shift=shift: e.tensor_scalar_add(out=t0[:], in0=thq[:], scalar1=shift), reads=["thq"], writes=[k0])
                    op("dve", lambda e, t0=t0, t1_=t1_: e.tensor_scalar_mul(out=t1_[:], in0=t0[:], scalar1=1.0 / (2 * math.pi)), reads=[k0], writes=[k1])
                    op("dve", lambda e, t1_=t1_: e.tensor_copy(qi[:], t1_[:]), reads=[k1], writes=["qi"])
                    op("dve", lambda e, t1_=t1_: e.tensor_copy(t1_[:], qi[:]), reads=["qi"], writes=[k1])
                    op("dve", lambda e, t0=t0, t1_=t1_: e.scalar_tensor_tensor(out=t0[:], in0=t1_[:], scalar=-2 * math.pi, in1=t0[:],
                                                                               op0=ALU.mult, op1=ALU.add), reads=[k0, k1], writes=[k0])
                    op("dve", lambda e, t0=t0, t1_=t1_: e.tensor_scalar(out=t1_[:], in0=t0[:], scalar1=math.pi, scalar2=2 * math.pi,
                                                                        op0=ALU.is_gt, op1=ALU.mult), reads=[k0], writes=[k1])
                    op("dve", lambda e, t0=t0, t1_=t1_: e.tensor_sub(t0[:], t0[:], t1_[:]), reads=[k0, k1], writes=[k0])
                    op("dve", lambda e, t0=t0, t1_=t1_: e.tensor_scalar(out=t1_[:], in0=t0[:], scalar1=-math.pi, scalar2=2 * math.pi,
                                                                        op0=ALU.is_lt, op1=ALU.mult), reads=[k0], writes=[k1])
                    op("dve", lambda e, t0=t0, t1_=t1_: e.tensor_add(t0[:], t0[:], t1_[:]), reads=[k0, k1], writes=[k0])
                    op("act", lambda e, t0=t0, idx=idx: e.activation(out=cs[:, idx, :], in_=t0[:], func=AF.Sin), reads=[k0], writes=["cs"])
                op("dve", lambda e: e.memset(PR[:, 0, :], 1.0), writes=["PR"])
                op("dve", lambda e: e.memset(PI[:, 0, :], 0.0), writes=["PI"])
                op("dve", lambda e: e.tensor_mul(PR[:, 1, :], mag[:], cs[:, 0, :]), reads=["mag", "cs"], writes=["PR"])
                op("dve", lambda e: e.tensor_mul(PI[:, 1, :], mag[:], cs[:, 1, :]), reads=["mag", "cs"], writes=["PI"])
                op("dve", lambda e: e.tensor_copy(P1r[:], PR[:, 1, :]), reads=["PR"], writes=["P1r"])
                op("dve", lambda e: e.tensor_copy(P1i[:], PI[:, 1, :]), reads=["PI"], writes=["P1i"])
                op("dve", lambda e: e.memset(VR[:, 0, :], 1.0), writes=["VR"])
                op("dve", lambda e: e.memset(VI[:, 0, :], 0.0), writes=["VI"])
                op("dve", lambda e: e.tensor_mul(VR[:, 1, :], PR[:, 1, :], imag[:]), reads=["PR", "imag"], writes=["VR"])
                op("dve", lambda e: e.scalar_tensor_tensor(out=VI[:, 1, :], in0=PI[:, 1, :], scalar=-1.0, in1=imag[:],
                                                           op0=ALU.mult, op1=ALU.mult), reads=["PI", "imag"], writes=["VI"])

                tq = [sb([128, 1024], F32, "tq%d" % i, su) for i in range(2)]
                tqp = [sb([128, 2048], F32, "tqp%d" % i, su) for i in range(2)]

                def cmul(outr, outi, ar, ai, br, bi, shape, rk, wk, eng="dve"):
                    n = 1
                    for v in shape[1:]:
                        n *= v
                    tset, tkeys = (tq, ["tq0", "tq1"]) if eng == "dve" else (tqp, ["tqp0", "tqp1"])
                    t0 = tset[0][:, 0:n]
                    t1 = tset[1][:, 0:n]
                    if len(shape) == 3:
                        t0 = t0.rearrange("p (a b) -> p a b", b=shape[2])
                        t1 = t1.rearrange("p (a b) -> p a b", b=shape[2])
                    elif len(shape) == 4:
                        t0 = t0.rearrange("p (a b c) -> p a b c", b=shape[2], c=shape[3])
                        t1 = t1.rearrange("p (a b c) -> p a b c", b=shape[2], c=shape[3])
                    op(eng, lambda e: e.tensor_mul(t0, ar, br), reads=rk, writes=[tkeys[0]])
                    op(eng, lambda e: e.tensor_mul(t1, ai, bi), reads=rk, writes=[tkeys[1]])
                    op(eng, lambda e: e.tensor_sub(outr, t0, t1), reads=tkeys, writes=wk)
                    op(eng, lambda e: e.tensor_mul(t0, ar, bi), reads=rk, writes=[tkeys[0]])
                    op(eng, lambda e: e.tensor_mul(t1, ai, br), reads=rk, writes=[tkeys[1]])
                    op(eng, lambda e: e.tensor_add(outi, t0, t1), reads=tkeys, writes=wk)

                for (TR, TI, nmk) in ((PR, PI, ["PR", "PI"]), (VR, VI, ["VR", "VI"])):
                    for n in (1, 2, 4):
                        cmul(TR[:, n + 1:2 * n + 1, :], TI[:, n + 1:2 * n + 1, :], TR[:, 1:n + 1, :], TI[:, 1:n + 1, :],
                             TR[:, n:n + 1, :].to_broadcast([128, n, 16]), TI[:, n:n + 1, :].to_broadcast([128, n, 16]),
                             [128, n, 16], nmk, nmk)
                zr = tmpq[0]
                zi = tmpq[1]
                den = tmpq[2]
                t3 = tmpq[3]
                t4 = tmpq[4]
                op("dve", lambda e: e.tensor_mul(den[:], lamr[:], lamr[:]), reads=["lamr"], writes=["tmpq2"])
                op("dve", lambda e: e.tensor_mul(t3[:], lami[:], lami[:]), reads=["lami"], writes=["tmpq3"])
                op("dve", lambda e: e.tensor_add(den[:], den[:], t3[:]), reads=["tmpq2", "tmpq3"], writes=["tmpq2"])
                op("dve", lambda e: e.reciprocal(den[:], den[:]), reads=["tmpq2"], writes=["tmpq2"])
                op("dve", lambda e: e.tensor_scalar_add(out=t4[:], in0=cs[:, 0, :], scalar1=-1.0), reads=["cs"], writes=["tmpq4"])
                op("dve", lambda e: e.tensor_mul(t3[:], em1[:], cs[:, 0, :]), reads=["em1", "cs"], writes=["tmpq3"])
                op("dve", lambda e: e.tensor_add(t4[:], t4[:], t3[:]), reads=["tmpq4", "tmpq3"], writes=["tmpq4"])
                op("dve", lambda e: e.tensor_mul(zr[:], t4[:], lamr[:]), reads=["tmpq4", "lamr"], writes=["tmpq0"])
                op("dve", lambda e: e.tensor_mul(t3[:], PI[:, 1, :], lami[:]), reads=["PI", "lami"], writes=["tmpq3"])
                op("dve", lambda e: e.tensor_add(zr[:], zr[:], t3[:]), reads=["tmpq0", "tmpq3"], writes=["tmpq0"])
                op("dve", lambda e: e.tensor_mul(zr[:], zr[:], den[:]), reads=["tmpq0", "tmpq2"], writes=["tmpq0"])
                op("dve", lambda e: e.tensor_mul(zi[:], PI[:, 1, :], lamr[:]), reads=["PI", "lamr"], writes=["tmpq1"])
                op("dve", lambda e: e.tensor_mul(t3[:], t4[:], lami[:]), reads=["tmpq4", "lami"], writes=["tmpq3"])
                op("dve", lambda e: e.tensor_sub(zi[:], zi[:], t3[:]), reads=["tmpq1", "tmpq3"], writes=["tmpq1"])
                op("dve", lambda e: e.tensor_mul(zi[:], zi[:], den[:]), reads=["tmpq1", "tmpq2"], writes=["tmpq1"])
                Bb = [sb([128, 16, 16], F32, "Bb%d" % i, su) for i in range(2)]
                cmul(Bb[0][:], Bb[1][:], Bq[0][:], Bq[1][:], zr[:].unsqueeze(2).to_broadcast([128, 16, 16]),
                     zi[:].unsqueeze(2).to_broadcast([128, 16, 16]), [128, 16, 16], ["Bq0", "Bq1", "tmpq0", "tmpq1"], ["Bb0", "Bb1"])
                for part in range(2):
                    for T in range(4):
                        for gl in range(2):
                            rhs = ident32[:, :].rearrange("p (a b) -> p a b", b=32)[:, :, 16 * gl:16 * gl + 16]
                            last = (T == 3 and gl == 1)
                            op("pe", lambda e, part=part, T=T, gl=gl, rhs=rhs: e.matmul(
                                pb[part][gl * 64:(gl + 1) * 64, T * 64:(T + 1) * 64].rearrange("p (a b) -> p a b", b=16), Cnat[part][:, T, :], rhs, start=True, stop=True),
                               reads=["Cnat%d" % part, "ident32"], writes=["pb%d" % part], sig=last)
                    op("dve", lambda e, part=part: e.tensor_copy(Cq[part][:].rearrange("p j h -> p (j h)"), pb[part][:, 0:256]),
                       reads=["pb%d" % part], writes=["Cq%d" % part])
                Wq = [sb([128, 16, 8, 16], F32, "Wq%d" % i, su) for i in range(2)]
                Xq = [sb([128, 16, 8, 16], F32, "Xq%d" % i, su) for i in range(2)]
                BPq = [sb([128, 16, 8, 16], F32, "BPq%d" % i, su) for i in range(2)]
                mask32 = sb([128, 128], F32, "mask32", su)
                op("pool", lambda e: e.memset(mask32[:], 1.0), writes=["mask32"])
                op("pool", lambda e: e.affine_select(out=mask32[:].rearrange("p (t h) -> p t h", h=16),
                                                     in_=mask32[:].rearrange("p (t h) -> p t h", h=16),
                                                     pattern=[[16, 8], [0, 16]], compare_op=ALU.is_ge, fill=0.0, base=15,
                                                     channel_multiplier=-1), reads=["mask32"], writes=["mask32"])
                tmpT = [sb([128, 4, 128], F32, "tmpT%d" % i, su) for i in range(2)]
                BPb = [sb([128, 16, 128], BF16, "BPb%d" % i, su) for i in range(2)]
                HK = lambda nm, jh: ["%s0_h%d" % (nm, jh), "%s1_h%d" % (nm, jh)]
                for jh in range(2):
                    js = slice(jh * 8, jh * 8 + 8)
                    b4h = lambda t_: t_[:, js, :].unsqueeze(2).to_broadcast([128, 8, 8, 16])
                    pwh = lambda T_: T_[:, 1:9, js].rearrange("p s j -> p j s").unsqueeze(3).to_broadcast([128, 8, 8, 16])
                    cmul(Xq[0][:, js], Xq[1][:, js], b4h(Cq[0]), b4h(Cq[1]), pwh(PR), pwh(PI), [128, 8, 8, 16], ["Cq0", "Cq1", "PR", "PI"], HK("Xq", jh))
                    op("dve", lambda e: e.tensor_scalar_mul(out=Xq[1][:, js], in0=Xq[1][:, js], scalar1=-1.0),
                       reads=HK("Xq", jh), writes=[HK("Xq", jh)[1]])
                for jh in range(2):
                    js = slice(jh * 8, jh * 8 + 8)
                    b4h = lambda t_: t_[:, js, :].unsqueeze(2).to_broadcast([128, 8, 8, 16])
                    pwh = lambda T_: T_[:, 1:9, js].rearrange("p s j -> p j s").unsqueeze(3).to_broadcast([128, 8, 8, 16])
                    p8h = lambda T_: T_[:, 8, js].unsqueeze(2).unsqueeze(3).to_broadcast([128, 8, 8, 16])
                    cmul(Wq[0][:, js], Wq[1][:, js], b4h(Bb[0]), b4h(Bb[1]), pwh(VR), pwh(VI), [128, 8, 8, 16], ["Bb0", "Bb1", "VR", "VI"], HK("Wq", jh), eng="pool")
                    cmul(BPq[0][:, js], BPq[1][:, js], Wq[0][:, js], Wq[1][:, js], p8h(PR), p8h(PI), [128, 8, 8, 16], HK("Wq", jh) + ["PR", "PI"], HK("BPq", jh), eng="pool")
                for jh in range(2):
                    js = slice(jh * 8, jh * 8 + 8)
                    for part in range(2):
                        op("act", lambda e, part=part: e.activation(out=cpow_sb[:, js, part, :], in_=Xq[part][:, js].rearrange("p j t h -> p j (t h)"),
                                                                    func=AF.Copy), reads=[HK("Xq", jh)[part]], writes=["cpow_sb"])
                        op("act", lambda e, part=part: e.activation(out=BPb[part][:, js], in_=BPq[part][:, js].rearrange("p j s h -> p j (s h)"), func=AF.Copy),
                           reads=[HK("BPq", jh)[part]], writes=[HK("BPb", jh)[part]])
                    for jb in (2 * jh, 2 * jh + 1):
                        for gl in range(2):
                            bank = pb[4 + gl]
                            bk = "pb%d" % (4 + gl)
                            rows = slice(gl * 64, gl * 64 + 64)
                            for jq in range(4):
                                J = jb * 4 + jq
                                for part in range(2):
                                    op("pe", lambda e, J=J, rows=rows, jq=jq, part=part, bank=bank: e.matmul(
                                        bank[:, jq * 128 + part * 64:jq * 128 + part * 64 + 64], BPb[part][rows, J, :], identb[rows, rows],
                                        start=True, stop=True),
                                       reads=[HK("BPb", jh)[part], "identb"], writes=[bk], sig=(jq == 3 and part == 1))
                            gst = 2 * (jb * 4) + gl
                            op("act", lambda e, gst=gst, bank=bank: e.activation(out=bpow_sb[:, gst:gst + 7:2, :, :].rearrange("p g a b -> p g (a b)"),
                                                                             in_=bank[:].rearrange("p (g c) -> p g c", c=128), func=AF.Copy),
                               reads=[bk], writes=["bpow_sb"])
                    for jb in (2 * jh, 2 * jh + 1):
                        for gl in range(2):
                            bank = pb[2 + gl]
                            bk = "pb%d" % (2 + gl)
                            rows = slice(gl * 64, gl * 64 + 64)
                            for jq in range(4):
                                J = jb * 4 + jq
                                op("pe", lambda e, J=J, rows=rows, bank=bank, jq=jq: e.matmul(
                                    bank[:, jq * 128:(jq + 1) * 128], Wq[0][rows, J, :, :].rearrange("p s h -> p (s h)"),
                                    Xq[0][rows, J, :, :].rearrange("p t h -> p (t h)"), start=True, stop=False),
                                   reads=[HK("Wq", jh)[0], HK("Xq", jh)[0]], writes=[bk], sig=False)
                                op("pe", lambda e, J=J, rows=rows, bank=bank, jq=jq: e.matmul(
                                    bank[:, jq * 128:(jq + 1) * 128], Wq[1][rows, J, :, :].rearrange("p s h -> p (s h)"),
                                    Xq[1][rows, J, :, :].rearrange("p t h -> p (t h)"), start=False, stop=True),
                                   reads=[HK("Wq", jh)[1], HK("Xq", jh)[1]], writes=[bk], sig=(jq == 3))
                            tt_ = tmpT[gl]
                            tk = "tmpT%d" % gl
                            op("dve", lambda e, bank=bank, tt_=tt_: e.tensor_tensor(out=tt_[:], in0=bank[:].rearrange("p (g c) -> p g c", c=128),
                                                                                  in1=mask32[:].unsqueeze(1).to_broadcast([128, 4, 128]), op=ALU.mult),
                               reads=[bk, "mask32"], writes=[tk])
                            for jq in range(4):
                                g = 2 * (jb * 4 + jq) + gl
                                op("dve", lambda e, tt_=tt_, g=g, jq=jq: e.scalar_tensor_tensor(out=toep_sb[:, g, :], in0=ident32[:], scalar=Dcol[:, g:g + 1],
                                                                                              in1=tt_[:, jq, :], op0=ALU.mult, op1=ALU.add),
                                   reads=[tk, "ident32", "Dcol"], writes=["toep_sb"])
                op("dve", lambda e: e.tensor_mul(ER[:, :, 0], PR[:, 8, :], m8[:]), reads=["PR", "tmpq5"], writes=["ER"])
                op("dve", lambda e: e.scalar_tensor_tensor(out=EI[:, :, 0], in0=PI[:, 8, :], scalar=-1.0, in1=m8[:],
                                                           op0=ALU.mult, op1=ALU.mult), reads=["PI", "tmpq5"], writes=["EI"])
                for k in range(7):
                    n = 1 << k
                    cmul(ER[:, :, n:2 * n], EI[:, :, n:2 * n], ER[:, :, 0:n], EI[:, :, 0:n],
                         ER[:, :, n - 1:n].to_broadcast([128, 16, n]), EI[:, :, n - 1:n].to_broadcast([128, 16, n]),
                         [128, 16, n], ["ER", "EI"], ["ER", "EI"])
                barrier()
            op("dve", lambda e: e.memset(carry[:], 0.0), writes=["carry0", "carry1", "carry2", "carry3"])
            if stop == "setup":
                kb.halted = True

            pa = ExitStack()
            xa = [sb([128, D], F32, "xa%d" % i, pa) for i in range(3)]
            xn = [sb([128, D], BF16, "xn%d" % i, pa) for i in range(3)]
            arA = sb([128, 8192], BF16, "arA", pa)
            xnT = arA[:].rearrange("p (k s c) -> p k s c", k=8, s=8)
            ygU = arA[:, 0:4096].rearrange("p (t c) -> p t c", t=8)
            ygfm = arA[:, 4096:8192].rearrange("p (ct t c) -> p ct t c", ct=4, t=8)
            arB = sb([128, 4096], BF16, "arB", pa)
            U_sb = arB[:].rearrange("p (g s h) -> p g s h", g=32, s=8)
            Hsb = arB[:].rearrange("p (a j c) -> p a j c", a=2, j=16)
            ub = sb([128, 4, 16 + 1024], F32, "ub", pa)
            wa = sb([128, 1040], F32, "wa", pa)
            wb = sb([128, 1040], F32, "wb", pa)
            arC = sb([128, 1024], F32, "arC", pa)
            arD = sb([128, 1024], F32, "arD", pa)
            Gm = [arC[:, i * 512:(i + 1) * 512].rearrange("p (j c) -> p j c", j=4) for i in range(2)]
            ta = [arD[:, i * 512:(i + 1) * 512].rearrange("p (j c) -> p j c", j=4) for i in range(2)]
            pooled = [sb([128, 8, 128], BF16, "pooled%d" % i, pa) for i in range(2)]
            Ug = sb([128, 32, 128], BF16, "Ug", pa)
            rr_all = [[sb([128, 4, 128], F32, "rr%d_%d" % (k_, i), pa) for i in range(2)] for k_ in range(1)] * 2
            Hs_all = [[sb([128, 4, 129], F32, "Hs%d_%d" % (k_, i), pa) for i in range(2)] for k_ in range(1)] * 2
            rr = rr_all[0]
            Hs = Hs_all[0]
            th = [sb([128, 512], BF16, "th%d" % i, pa) for i in range(2)]
            fix16 = sb([128, 16], F32, "fix16", pa)
            tp_ = [sb([128, 4, 128], F32, "tp%d" % i, pa) for i in range(2)]

            UBK = ["ub0", "ub1", "ub2", "ub3"]
            op("dve", lambda e: e.memset(ub[:, :, 0:16], 0.0), writes=UBK)

            def s5_core(nch, Ug_ap, hs_src, ncolY, yg_out_fn, ugk, hsk):
                for gb in range(8):
                    bank = pb[gb % 2]
                    bk = "pb%d" % (gb % 2)
                    for gq in range(4):
                        g = gb * 4 + gq
                        J, gl = g // 2, g % 2
                        rows = slice(gl * 64, gl * 64 + 64)
                        o = bank[0:nch, gq * ncolY:(gq + 1) * ncolY]
                        op("pe", lambda e, o=o, g=g: e.matmul(o, Ug_ap(g), toep_sb[:, g, 0:ncolY], start=True, stop=False),
                           reads=[ugk, "toep_sb"], writes=[bk], sig=False)
                        op("pe", lambda e, o=o, J=J, gl=gl, rows=rows: e.matmul(o, hs_src(0, J, gl), cpow_sb[rows, J, 0, 0:ncolY],
                                                                                start=False, stop=False),
                           reads=[hsk, "cpow_sb"], writes=[bk], sig=False)
                        op("pe", lambda e, o=o, J=J, gl=gl, rows=rows: e.matmul(o, hs_src(1, J, gl), cpow_sb[rows, J, 1, 0:ncolY],
                                                                                start=False, stop=True),
                           reads=[hsk, "cpow_sb"], writes=[bk], sig=(gq == 3))
                    yg_out_fn(gb, bank, bk)

            for S in range(2):
                def ab1(s_):
                    i = s_ % 3
                    src = xp[1024 * S:1024 * (S + 1), :].rearrange("(c s) d -> s c d", s=8)[s_]
                    dma("sp", xa[i][:], src, writes=["xa%d" % i], sem="xa%d" % i)
                    return rstd_from([(xa[i][:], "xa%d" % i)], 128, None)

                def ab2(s_, r_ap, r_k):
                    i = s_ % 3
                    op("dve", lambda e: e.scalar_tensor_tensor(out=xn[i][:], in0=xa[i][:], scalar=r_ap, in1=g0[:],
                                                               op0=ALU.mult, op1=ALU.mult),
                       reads=["xa%d" % i, r_k, "g0"], writes=["xn%d" % i])
                    tb = ptb[s_ % 2]
                    tk = "pb%d" % (6 + s_ % 2)
                    for kt in range(8):
                        op("pe", lambda e, kt=kt: e.transpose(tb[:, kt * 128:(kt + 1) * 128], xn[i][:, kt * 128:(kt + 1) * 128], identb[:]),
                           reads=["xn%d" % i, "identb"], writes=[tk], sig=(kt == 7))
                    op("act", lambda e: e.activation(out=xnT[:, :, s_, :], in_=tb[:].rearrange("p (k c) -> p k c", c=128), func=AF.Copy),
                       reads=[tk], writes=["xnTs%d" % s_, "arA"])

                def ab3(s_):
                    bank = pb[s_ % 2]
                    bk = "pb%d" % (s_ % 2)
                    for kt in range(8):
                        op("pe", lambda e, kt=kt: e.matmul(bank[:], xnT[:, kt, s_, :], w_in_sb[:, kt, 0:512], start=(kt == 0), stop=(kt == 7)),
                           reads=["xnTs%d" % s_, "w_in_sb"], writes=[bk], sig=(kt == 7))
                    op("dve", lambda e: e.tensor_copy(U_sb[:, :, s_, :], bank[:].rearrange("p (g h) -> p g h", h=16)),
                       reads=[bk], writes=["arB"])

                rq = {0: ab1(0), 1: ab1(1)}
                ab2(0, *rq[0])
                for s in range(8):
                    if s + 2 < 8:
                        rq[s + 2] = ab1(s + 2)
                    if s + 1 < 8:
                        ab2(s + 1, *rq[s + 1])
                    ab3(s)
                for gi in range(4):
                    for nh in range(2):
                        bank = pb[2 + nh]
                        bk = "pb%d" % (2 + nh)
                        for kt in range(8):
                            op("pe", lambda e, gi=gi, nh=nh, kt=kt, bank=bank: e.matmul(
                                bank[:], w_in_sb[:, kt, 512 + gi * 128:512 + (gi + 1) * 128],
                                xnT[:, kt, nh * 4:(nh + 1) * 4, :].rearrange("p s c -> p (s c)"), start=(kt == 0), stop=(kt == 7)),
                               reads=["arA", "w_in_sb"], writes=[bk], sig=(kt == 7))
                        dst = ub[:, gi, 16:1040].rearrange("p (c s) -> p s c", s=8)[:, nh * 4:(nh + 1) * 4, :]
                        if nh == 0:
                            op("act", lambda e, dst=dst, bank=bank: e.activation(out=dst, in_=bank[:].rearrange("p (s c) -> p s c", c=128), func=AF.Copy),
                               reads=[bk], writes=["ub%d" % gi])
                        else:
                            op("dve", lambda e, dst=dst, bank=bank: e.tensor_copy(dst, bank[:].rearrange("p (s c) -> p s c", c=128)),
                               reads=[bk], writes=["ub%d" % gi])
                dcur = {}

                def d1(gi):
                    w = POOL_W[gi]
                    a_ = ub[:, gi, :]
                    lv = [(wa, "wa", 1), (wb, "wb", 2), (wa, "wa", 4), (wb, "wb", 8)]
                    nlev = int(math.log2(w))
                    cur, curk = a_, "ub%d" % gi
                    for li in range(nlev):
                        o_, ok, sh = lv[li]
                        lo = 2 * sh - 1
                        src_ = cur
                        op("dve", lambda e, o_=o_, src_=src_, lo=lo, sh=sh: e.tensor_add(o_[:, lo:1040], src_[:, lo:1040], src_[:, lo - sh:1040 - sh]),
                           reads=[curk], writes=[ok])
                        cur, curk = o_, ok
                    dcur[gi] = (cur, curk)

                def d2(gi):
                    w = POOL_W[gi]
                    a_ = ub[:, gi, :]
                    cur, curk = dcur[gi]
                    pl = pooled[gi % 2]
                    pk = "pooled%d" % (gi % 2)
                    op("dve", lambda e: e.scalar_tensor_tensor(
                        out=pl[:], in0=cur[:, 16:1040].rearrange("p (c s) -> p s c", s=8), scalar=1.0 / w,
                        in1=a_[:, 16:1040].rearrange("p (c s) -> p s c", s=8), op0=ALU.mult, op1=ALU.subtract),
                       reads=[curk, "ub%d" % gi], writes=[pk])
                    if S == 0:
                        op("dve", lambda e: e.tensor_mul(fix16[:], cur[:, 16:32], invcnt[:, gi, :]),
                           reads=[curk, "invcnt"], writes=["fix16"])
                        op("dve", lambda e: e.tensor_sub(pl[:, :, 0:2], fix16[:].rearrange("p (c s) -> p s c", s=8),
                                                         a_[:, 16:32].rearrange("p (c s) -> p s c", s=8)),
                           reads=["fix16", "ub%d" % gi], writes=[pk])
                    for nh in range(2):
                        bank = pb[4 + nh]
                        bk = "pb%d" % (4 + nh)
                        op("pe", lambda e, nh=nh, bank=bank: e.matmul(
                            bank[:], poolw_sb[:, gi, :], pl[:, nh * 4:(nh + 1) * 4, :].rearrange("p s c -> p (s c)"), start=True, stop=True),
                           reads=[pk, "poolw_sb"], writes=[bk])
                        op("act", lambda e, nh=nh, bank=bank: e.activation(
                            out=ycat[:, 4 + gi, 1024 * S + 512 * nh:1024 * S + 512 * (nh + 1)], in_=bank[:], func=AF.Copy, scale=pscale[:, gi:gi + 1]),
                           reads=[bk, "pscale"], writes=["ycat"])

                for gb in range(4):
                    tb = ptb[gb % 2]
                    tk = "pb%d" % (6 + gb % 2)
                    for gq in range(8):
                        g = gb * 8 + gq
                        op("pe", lambda e, g=g, gq=gq, tb=tb: e.transpose(tb[:, gq * 128:(gq + 1) * 128],
                                                                           U_sb[:, g, :, :].rearrange("p s h -> p (s h)"), identb[:]),
                           reads=["arB", "identb"], writes=[tk], sig=(gq == 7))
                    op("act", lambda e, gb=gb, tb=tb: e.activation(out=Ug[:, gb * 8:(gb + 1) * 8, :], in_=tb[:].rearrange("p (g c) -> p g c", c=128),
                                                                 func=AF.Copy), reads=[tk], writes=["Ug"])
                for pbt in range(4):
                    rr = rr_all[pbt % 2]
                    Hs = Hs_all[pbt % 2]
                    rrk = ["rr0_%d" % i for i in range(2)]
                    hsk = ["Hs0_%d" % i for i in range(2)]
                    for part in range(2):
                        bank = pb[4 + part]
                        bk = "pb%d" % (4 + part)
                        for j in range(4):
                            J = pbt * 4 + j
                            for gl in range(2):
                                g = 2 * J + gl
                                op("pe", lambda e, bank=bank, j=j, gl=gl, g=g, part=part: e.matmul(
                                    bank[gl * 64:(gl + 1) * 64, j * 128:(j + 1) * 128], bpow_sb[:, g, part, :], Ug[:, g, :], start=True, stop=True),
                                   reads=["bpow_sb", "Ug"], writes=[bk], sig=(j == 3 and gl == 1))
                    Js = slice(pbt * 4, pbt * 4 + 4)
                    Gr = pb[4][:].rearrange("p (j c) -> p j c", c=128)
                    Gi = pb[5][:].rearrange("p (j c) -> p j c", c=128)
                    tt = lambda o_, ok, a_, ak, b_, bkk, fn="tensor_mul": op(
                        "dve", lambda e: getattr(e, fn)(o_, a_, b_), reads=[ak, bkk], writes=[ok])
                    tt(ta[0][:], "arD", ER[:, Js, :], "ER", Gr, "pb4")
                    tt(ta[1][:], "arD", EI[:, Js, :], "EI", Gi, "pb5")
                    tt(Gm[0][:], "arC", ta[0][:], "arD", ta[1][:], "arD", "tensor_sub")
                    tt(ta[0][:], "arD", ER[:, Js, :], "ER", Gi, "pb5")
                    tt(ta[1][:], "arD", EI[:, Js, :], "EI", Gr, "pb4")
                    tt(Gm[1][:], "arC", ta[0][:], "arD", ta[1][:], "arD", "tensor_add")
                    for part in range(2):
                        for j in range(4):
                            J = pbt * 4 + j
                            op("dve", lambda e, part=part, j=j, J=J: e.tensor_tensor_scan(
                                out=rr[part][:, j, :], data0=rho8[:, J:J + 1].to_broadcast([128, 128]), data1=Gm[part][:, j, :],
                                initial=carry[:, part, J:J + 1], op0=ALU.mult, op1=ALU.add),
                               reads=["arC", "rho8", "carry%d" % pbt], writes=[rrk[part]], sig=(j == 3))
                    for part in range(2):
                        op("dve", lambda e, part=part, Js=Js: e.tensor_copy(Hs[part][:, :, 0], carry[:, part, Js]),
                           reads=["carry%d" % pbt], writes=[hsk[part]])
                    tp2 = lambda eng_, o_, ok, a_, ak, b_, bkk, fn="tensor_mul": op(
                        eng_, lambda e: getattr(e, fn)(o_, a_, b_), reads=[ak, bkk], writes=[ok])
                    tp2("dve", tp_[0][:], "tp0", ER[:, Js, :], "ER", rr[0][:], rrk[0])
                    tp2("dve", tp_[1][:], "tp1", EI[:, Js, :], "EI", rr[1][:], rrk[1])
                    tp2("dve", Hs[0][:, :, 1:129], hsk[0], tp_[0][:], "tp0", tp_[1][:], "tp1", "tensor_add")
                    tp2("dve", ta[0][:], "arD", ER[:, Js, :], "ER", rr[1][:], rrk[1])
                    tp2("dve", ta[1][:], "arD", EI[:, Js, :], "EI", rr[0][:], rrk[0])
                    tp2("dve", Hs[1][:, :, 1:129], hsk[1], ta[0][:], "arD", ta[1][:], "arD", "tensor_sub")
                    for part in range(2):
                        op("dve", lambda e, part=part, Js=Js: e.tensor_copy(carry[:, part, Js], Hs[part][:, :, 128]),
                           reads=[hsk[part]], writes=["carry%d" % pbt])
                        op("act", lambda e, part=part, Js=Js: e.activation(out=Hsb[:, part, Js, :], in_=Hs[part][:, :, 0:128], func=AF.Copy),
                           reads=[hsk[part]], writes=["arB"])

                def yg_out(gb, bank, bk):
                    dst = ygU[:].rearrange("p t (g h) -> p g t h", h=16)[:, gb * 4:(gb + 1) * 4, :, :]
                    op("act", lambda e: e.activation(out=dst, in_=bank[:].rearrange("p (g t h) -> p g t h", t=8, h=16), func=AF.Gelu_apprx_tanh),
                       reads=[bk], writes=["arA"])

                d1(0)
                s5_core(128, lambda g: Ug[:, g, :], lambda part, J, gl: Hsb[gl * 64:(gl + 1) * 64, part, J, :], 128, yg_out, "Ug", "arB")
                d2(0)
                d1(1)
                for tp in range(4):
                    tb = ptb[tp % 2]
                    tk = "pb%d" % (6 + tp % 2)
                    for t2 in range(2):
                        t = tp * 2 + t2
                        for ct in range(4):
                            op("pe", lambda e, t=t, t2=t2, ct=ct, tb=tb: e.transpose(tb[:, (t2 * 4 + ct) * 128:(t2 * 4 + ct + 1) * 128],
                                                                                  ygU[:, t, ct * 128:(ct + 1) * 128], identb[:]),
                               reads=["arA", "identb"], writes=[tk], sig=(t2 == 1 and ct == 3))
                    op("act", lambda e, tp=tp, tb=tb: e.activation(out=ygfm[:, :, tp * 2:tp * 2 + 2, :].rearrange("p ct t c -> p t ct c"),
                                                                 in_=tb[:].rearrange("p (t ct c) -> p t ct c", ct=4, c=128), func=AF.Copy),
                       reads=[tk], writes=["arA"])
                d2(1)
                d1(2)
                for ct in range(4):
                    for nh in range(2):
                        bank = pb[2 + nh]
                        bk = "pb%d" % (2 + nh)
                        gsrc = ygfm[:, ct, nh * 4:(nh + 1) * 4, :].rearrange("p t c -> p (t c)")
                        op("pe", lambda e, ct=ct, bank=bank, gsrc=gsrc: e.matmul(bank[:], glu_sb[:, ct, :], gsrc, start=True, stop=True),
                           reads=["arA", "glu_sb"], writes=[bk])
                        op("act", lambda e, nh=nh, bank=bank: e.activation(out=th[nh][:], in_=bank[:], func=AF.Tanh, scale=0.5),
                           reads=[bk], writes=["th%d" % nh])
                        op("dve", lambda e, ct=ct, nh=nh, gsrc=gsrc, S=S: e.scalar_tensor_tensor(
                            out=ycat[:, ct, 1024 * S + 512 * nh:1024 * S + 512 * (nh + 1)], in0=th[nh][:], scalar=1.0, in1=gsrc,
                            op0=ALU.add, op1=ALU.mult), reads=["th%d" % nh, "arA"], writes=["ycat"])
                d2(2)
                d1(3)
                d2(3)
                if S == 1:
                    for gi in range(4):
                        dma("sp", o_pool_p[:, gi * 128:(gi + 1) * 128].rearrange("t c -> c t"), ub[:, gi, 1025:1040], reads=["ub%d" % gi], sem="out", is_out=True)
                op("dve", lambda e: e.tensor_copy(ub[:, :, 0:16], ub[:, :, 1024:1040]), reads=UBK, writes=UBK)
                if stop == "S%d" % S:
                    kb.halted = True
            dma("sp", o_re_p.rearrange("(j q) -> q j", q=128), carry[:, 0, :], reads=["carry0", "carry1", "carry2", "carry3"], sem="out", is_out=True)
            dma("sp", o_im_p.rearrange("(j q) -> q j", q=128), carry[:, 1, :], reads=["carry0", "carry1", "carry2", "carry3"], sem="out", is_out=True)

            R = NSMP
            if stop == "p1":
                kb.halted = True
            barrier()
            pa.close()
            with ExitStack() as sm:
                xas = sb([R, D], F32, "xas", sm)
                xns = sb([R, D], BF16, "xns", sm)
                xnTs = sb([128, 8, R], BF16, "xnTs", sm)
                tA = sb([128, 16, R], F32, "tA", sm)
                tB = sb([128, 16, R], F32, "tB", sm)
                u_s = sb([R, D], F32, "u_s", sm)
                stt_ = [sb([R, 2048], F32, "stt%d" % i, sm) for i in range(2)]
                h0 = [sb([128, 16, R], F32, "h0_%d" % i, sm) for i in range(2)]
                h0b = sb([128, 2, 16, R], BF16, "h0b", sm)
                hn_ = [sb([128, 16, R], F32, "hn_%d" % i, sm) for i in range(2)]
                Uz = sb([R, 32, 8, 16], BF16, "Uz", sm)
                Ugs = sb([128, 32, R], BF16, "Ugs", sm)
                ygs = sb([R, 512], BF16, "ygs", sm)
                ygfs = sb([128, 4, R], BF16, "ygfs", sm)
                ths = sb([128, R], BF16, "ths", sm)
                Ep = sb([R, 26, 128], F32, "Ep", sm)
                red = sb([R, 512], F32, "red", sm)
                pls = sb([R, 512], BF16, "pls", sm)
                plT = sb([128, 4, R], BF16, "plT", sm)
                sto = sb([R, 2048], F32, "sto", sm)

                dma("sp", xas[:], xs, writes=["xas"], sem="smp")
                dma("sp", stt_[0][:], st_re, writes=["stt0"], sem="smp")
                dma("sp", stt_[1][:], st_im, writes=["stt1"], sem="smp")
                roff = 0
                for gi, w in enumerate(POOL_W):
                    dma("sp", Ep[:, roff:roff + w - 1, :], st_pool[:, 15 - (w - 1):15, gi * 128:(gi + 1) * 128], writes=["Ep"], sem="smp")
                    roff += w - 1
                dma("sp", o_pool_s[:, 0:14, :], st_pool[:, 1:15, :], sem="out", is_out=True)
                r_ap, r_k = rstd_from([(xas[:], "xas")], R, None)
                op("dve", lambda e: e.scalar_tensor_tensor(out=xns[:], in0=xas[:], scalar=r_ap, in1=g0[0:R, :],
                                                           op0=ALU.mult, op1=ALU.mult), reads=["xas", r_k, "g0"], writes=["xns"])
                for kt in range(8):
                    op("pe", lambda e, kt=kt: e.transpose(ptb[0][:, kt * R:(kt + 1) * R], xns[:, kt * 128:(kt + 1) * 128], identb[0:R, 0:R]),
                       reads=["xns", "identb"], writes=["pb6"], sig=(kt == 7))
                op("act", lambda e: e.activation(out=xnTs[:], in_=ptb[0][:, 0:8 * R].rearrange("p (k c) -> p k c", c=R), func=AF.Copy),
                   reads=["pb6"], writes=["xnTs"])
                for half in range(2):
                    for kt in range(8):
                        op("pe", lambda e, half=half, kt=kt: e.matmul(pb[half][0:R, :], xnTs[:, kt, :], w_in_sb[:, kt, half * 512:(half + 1) * 512],
                                                                      start=(kt == 0), stop=(kt == 7)),
                           reads=["xnTs", "w_in_sb"], writes=["pb%d" % half], sig=(kt == 7))
                    op("dve", lambda e, half=half: e.tensor_copy(u_s[:, half * 512:(half + 1) * 512], pb[half][0:R, :]),
                       reads=["pb%d" % half], writes=["u_s"])
                dma("sp", o_pool_s[:, 14, :], u_s[:, 512:1024], reads=["u_s"], sem="out", is_out=True)
                roff = 0
                for gi, w in enumerate(POOL_W):
                    cols = slice(gi * 128, (gi + 1) * 128)
                    if w - 1 > 1:
                        op("dve", lambda e, roff=roff, w=w, cols=cols: e.tensor_reduce(
                            out=red[:, cols], in_=Ep[:, roff:roff + w - 1, :].rearrange("p r c -> p c r"), axis=AX.X, op=ALU.add),
                           reads=["Ep"], writes=["red"])
                    else:
                        op("dve", lambda e, roff=roff, cols=cols: e.tensor_copy(red[:, cols], Ep[:, roff, :]), reads=["Ep"], writes=["red"])
                    roff += w - 1
                    uc = u_s[:, 512 + gi * 128:512 + (gi + 1) * 128]
                    op("dve", lambda e, cols=cols, uc=uc: e.tensor_add(red[:, cols], red[:, cols], uc), reads=["red", "u_s"], writes=["red"])
                    op("dve", lambda e, cols=cols, uc=uc, w=w: e.scalar_tensor_tensor(out=pls[:, cols], in0=red[:, cols], scalar=1.0 / w, in1=uc,
                                                                                      op0=ALU.mult, op1=ALU.subtract),
                       reads=["red", "u_s"], writes=["pls"])
                for gi in range(4):
                    op("pe", lambda e, gi=gi: e.transpose(ptb[1][:, gi * R:(gi + 1) * R], pls[:, gi * 128:(gi + 1) * 128], identb[0:R, 0:R]),
                       reads=["pls", "identb"], writes=["pb7"], sig=(gi == 3))
                op("act", lambda e: e.activation(out=plT[:], in_=ptb[1][:, 0:4 * R].rearrange("p (g c) -> p g c", c=R), func=AF.Copy),
                   reads=["pb7"], writes=["plT"])
                for gi in range(4):
                    op("pe", lambda e, gi=gi: e.matmul(pb[2][:, gi * R:(gi + 1) * R], poolw_sb[:, gi, :], plT[:, gi, :], start=True, stop=True),
                       reads=["plT", "poolw_sb"], writes=["pb2"], sig=(gi == 3))
                for gi in range(4):
                    op("act", lambda e, gi=gi: e.activation(out=ycat[:, 4 + gi, SEQ:SEQ + R], in_=pb[2][:, gi * R:(gi + 1) * R], func=AF.Copy,
                                                            scale=pscale[:, gi:gi + 1]), reads=["pb2", "pscale"], writes=["ycat"])
                for part in range(2):
                    for J in range(16):
                        op("pe", lambda e, part=part, J=J: e.matmul(pb[3][:, (part * 16 + J) * R:(part * 16 + J + 1) * R],
                                                                    stt_[part][:, J * 128:(J + 1) * 128], ident32[0:R, 0:R], start=True, stop=True),
                           reads=["stt%d" % part, "ident32"], writes=["pb3"], sig=(J == 15))
                    op("dve", lambda e, part=part: e.tensor_copy(h0[part][:], pb[3][:, part * 16 * R:(part + 1) * 16 * R].rearrange("p (j b) -> p j b", b=R)),
                       reads=["pb3"], writes=["h0_%d" % part])
                    op("act", lambda e, part=part: e.activation(out=h0b[:, part, :, :], in_=h0[part][:], func=AF.Copy),
                       reads=["h0_%d" % part], writes=["h0b"])
                op("dve", lambda e: e.memset(Uz[:], 0.0), writes=["Uz"])
                for s in (0, 7):
                    op("dve", lambda e, s=s: e.tensor_copy(Uz[:, :, s, :], u_s[:, 0:512].rearrange("p (g h) -> p g h", h=16)),
                       reads=["u_s"], writes=["Uz"])
                for gb in range(4):
                    tb = ptb[gb % 2]
                    tk = "pb%d" % (6 + gb % 2)
                    for gq in range(8):
                        g = gb * 8 + gq
                        op("pe", lambda e, g=g, gq=gq, tb=tb: e.transpose(tb[:, gq * R:(gq + 1) * R], Uz[:, g, :, :].rearrange("p s h -> p (s h)"),
                                                                           identb[0:R, 0:R]), reads=["Uz", "identb"], writes=[tk], sig=(gq == 7))
                    op("act", lambda e, gb=gb, tb=tb: e.activation(out=Ugs[:, gb * 8:(gb + 1) * 8, :], in_=tb[:, 0:8 * R].rearrange("p (g c) -> p g c", c=R),
                                                                 func=AF.Copy), reads=[tk], writes=["Ugs"])
                for part in range(2):
                    bank = pb[4 + part]
                    bk = "pb%d" % (4 + part)
                    for g in range(32):
                        J, gl = g // 2, g % 2
                        op("pe", lambda e, bank=bank, J=J, gl=gl, g=g, part=part: e.matmul(
                            bank[gl * 64:(gl + 1) * 64, J * R:(J + 1) * R], bpow_sb[64:128, g, part, :], Ugs[64:128, g, :], start=True, stop=True),
                           reads=["bpow_sb", "Ugs"], writes=[bk], sig=(g == 31))
                b3 = lambda t_: t_[:].unsqueeze(2).to_broadcast([128, 16, R])
                Gsr = pb[4][:, 0:16 * R].rearrange("p (j b) -> p j b", b=R)
                Gsi = pb[5][:, 0:16 * R].rearrange("p (j b) -> p j b", b=R)
                t0 = tA[:]
                t1 = tB[:]
                op("dve", lambda e: e.tensor_mul(t0, h0[0][:], b3(P1r)), reads=["h0_0", "P1r"], writes=["tAB"])
                op("dve", lambda e: e.tensor_mul(t1, h0[1][:], b3(P1i)), reads=["h0_1", "P1i"], writes=["tAB"])
                op("dve", lambda e: e.tensor_sub(t0, t0, t1), reads=["tAB"], writes=["tAB"])
                op("dve", lambda e: e.tensor_add(hn_[0][:], t0, Gsr), reads=["tAB", "pb4"], writes=["hn_0"])
                op("dve", lambda e: e.tensor_mul(t0, h0[0][:], b3(P1i)), reads=["h0_0", "P1i"], writes=["tAB"])
                op("dve", lambda e: e.tensor_mul(t1, h0[1][:], b3(P1r)), reads=["h0_1", "P1r"], writes=["tAB"])
                op("dve", lambda e: e.tensor_add(t0, t0, t1), reads=["tAB"], writes=["tAB"])
                op("dve", lambda e: e.tensor_add(hn_[1][:], t0, Gsi), reads=["tAB", "pb5"], writes=["hn_1"])
                for part, o_ap in ((0, o_re_s), (1, o_im_s)):
                    for J in range(16):
                        bank = pb[J // 4]
                        bk = "pb%d" % (J // 4)
                        op("pe", lambda e, part=part, J=J, bank=bank: e.matmul(bank[0:R, (J % 4) * 128:(J % 4 + 1) * 128], hn_[part][:, J, :], ident32[:],
                                                                               start=True, stop=True),
                           reads=["hn_%d" % part, "ident32"], writes=[bk], sig=(J % 4 == 3))
                        if J % 4 == 3:
                            op("dve", lambda e, J=J, bank=bank: e.tensor_copy(sto[:, (J // 4) * 512:(J // 4 + 1) * 512], bank[0:R, :]),
                               reads=[bk], writes=["sto"])
                    dma("sp", o_ap, sto[:], reads=["sto"], sem="out", is_out=True)

                def yg_out_s(gb, bank, bk):
                    op("act", lambda e: e.activation(out=ygs[:, gb * 64:(gb + 1) * 64], in_=bank[0:R, 0:64], func=AF.Gelu_apprx_tanh),
                       reads=[bk], writes=["ygs"])

                s5_core(R, lambda g: Ugs[:, g, :], lambda part, J, gl: h0b[gl * 64:(gl + 1) * 64, part, J, :], 16, yg_out_s, "Ugs", "h0b")
                for ct in range(4):
                    op("pe", lambda e, ct=ct: e.transpose(ptb[0][:, ct * R:(ct + 1) * R], ygs[:, ct * 128:(ct + 1) * 128], identb[0:R, 0:R]),
                       reads=["ygs", "identb"], writes=["pb6"], sig=(ct == 3))
                op("act", lambda e: e.activation(out=ygfs[:], in_=ptb[0][:, 0:4 * R].rearrange("p (g c) -> p g c", c=R), func=AF.Copy),
                   reads=["pb6"], writes=["ygfs"])
                for ct in range(4):
                    op("pe", lambda e, ct=ct: e.matmul(pb[2][:, ct * R:(ct + 1) * R], glu_sb[:, ct, :], ygfs[:, ct, :], start=True, stop=True),
                       reads=["ygfs", "glu_sb"], writes=["pb2"])
                    op("act", lambda e, ct=ct: e.activation(out=ths[:], in_=pb[2][:, ct * R:(ct + 1) * R], func=AF.Tanh, scale=0.5),
                       reads=["pb2"], writes=["ths"])
                    op("dve", lambda e, ct=ct: e.scalar_tensor_tensor(out=ycat[:, ct, SEQ:SEQ + R], in0=ths[:], scalar=1.0, in1=ygfs[:, ct, :],
                                                                      op0=ALU.add, op1=ALU.mult), reads=["ths", "ygfs"], writes=["ycat"])
                if dbg and "ycat" in dbg:
                    dma("sp", dbg["ycat"], ycat[:], reads=["ycat"], sem="out", is_out=True)
                barrier()

        p1.close()
        if stop == "sample":
            kb.halted = True
        w_out_sb = sb([128, 8, D], BF16, "w_out_sb")
        w_up_sb = sb([128, 8, DFF], BF16, "w_up_sb")
        w_dn_sb = sb([128, 32, D], BF16, "w_dn_sb")
        gpost = sb([128, 2, D], F32, "gpost")
        g2col = sb([128, 8], F32, "g2col")
        dma("sp", gpost[:, 0, :], g_mix_post.partition_broadcast(128), writes=["gpost"])
        dma("sp", gpost[:, 1, :], g_mlp_post.partition_broadcast(128), writes=["gpost"])
        dma("sp", g2col[:], g_mlp_pre.rearrange("(k p) -> p k", p=128), writes=["g2col"])
        wov = w_out.rearrange("(kt p) n -> p kt n", p=128)
        for h_ in range(2):
            dma("pool", w_out_sb[:, 4 * h_:4 * h_ + 4, :], wov[:, 4 * h_:4 * h_ + 4, :], writes=["w_out_sb"])
        wuv = w_up.rearrange("(kt p) n -> p kt n", p=128)
        wdv = w_dn.rearrange("(ft p) n -> p ft n", p=128)
        for fc in range(8):
            dma("pool", w_up_sb[:, :, fc * 512:(fc + 1) * 512], wuv[:, :, fc * 512:(fc + 1) * 512], writes=["w_up%d" % fc])
            dma("pool", w_dn_sb[:, fc * 4:(fc + 1) * 4, :], wdv[:, fc * 4:(fc + 1) * 4, :], writes=["w_dn%d" % fc])
        op("dve", lambda e: e.tensor_scalar_mul(out=w_out_sb[:, 0:4, :], in0=w_out_sb[:, 0:4, :], scalar1=0.5),
           reads=["w_out_sb"], writes=["w_out_sb"])
        def w_scale(fc):
            for kt in range(8):
                op("dve", lambda e, kt=kt: e.tensor_scalar_mul(out=w_up_sb[:, kt, fc * 512:(fc + 1) * 512],
                                                               in0=w_up_sb[:, kt, fc * 512:(fc + 1) * 512], scalar1=g2col[:, kt:kt + 1]),
                   reads=["w_up%d" % fc, "g2col"], writes=["w_up%d" % fc])

        xr = [sb([128, D], F32, "xr%d" % i) for i in range(3)]
        hn = [sb([128, D], BF16, "hn%d" % i) for i in range(2)]
        hnT = sb([128, 8, 256], BF16, "hnT")
        ffT = [sb([128, 256], BF16, "ffT%d" % i) for i in range(4)]
        acc = pb[0:4]
        upb = [pb[4], pb[5]]
        v3 = lambda ap_: ap_.rearrange("p (a b) -> p a b", b=128)

        units = []
        for S in range(2):
            for s2 in range(4):
                units.append([(128, 1024 * S + 128 * (2 * s2 + i),
                               xp[1024 * S:1024 * (S + 1), :].rearrange("(c s) d -> s c d", s=8)[2 * s2 + i],
                               y_p[1024 * S:1024 * (S + 1), :].rearrange("(c s) d -> s c d", s=8)[2 * s2 + i]) for i in range(2)])
        units.append([(NSMP, SEQ, xs, y_s)])
        NU = len(units)

        free_slots = [("xr%d" % i, v3(xr[i][:])) for i in range(3)]
        slot_of = {}

        def ycat_slot(u):
            c0 = units[u][0][1]
            return ("yc%d" % u, ycat[:, :, c0:c0 + 256].bitcast(F32))

        def pre(u, i):
            rows, col0, xsrc, ydst = units[u][i]
            hk, H = free_slots.pop(0)
            slot_of[(u, i)] = (hk, H)
            Hh = [H[0:rows, 0:4, :], H[0:rows, 4:8, :]]
            m = [pb[6], pb[7]]
            mk = ["pb6", "pb7"]
            dma("sp", H[0:rows], xsrc.rearrange("r (a b) -> r a b", b=128), writes=[hk])
            for half in range(2):
                for ct in range(8):
                    op("pe", lambda e, half=half, ct=ct: e.matmul(
                        m[half][0:rows, :], ycat[:, ct, col0:col0 + rows], w_out_sb[:, ct, half * 512:(half + 1) * 512],
                        start=(ct == 0), stop=(ct == 7)), reads=["yc%d" % u, "w_out_sb"], writes=[mk[half]], sig=(ct == 7))
            r_ap, r_k = rstd_from([(m[h_][0:rows, :], mk[h_]) for h_ in range(2)], rows, None)
            for half in range(2):
                cs_ = slice(half * 512, (half + 1) * 512)
                op("dve", lambda e, half=half, cs_=cs_: e.scalar_tensor_tensor(
                    out=m[half][0:rows, :], in0=m[half][0:rows, :], scalar=r_ap, in1=gpost[0:rows, 0, cs_],
                    op0=ALU.mult, op1=ALU.mult), reads=[mk[half], r_k, "gpost"], writes=[mk[half]])
                op("dve", lambda e, half=half: e.tensor_add(Hh[half], v3(m[half][0:rows, :]), Hh[half]),
                   reads=[mk[half], hk], writes=[hk])
            cA, cB = stat_col(), stat_col()
            for half, c_ in ((0, cA), (1, cB)):
                op("dve", lambda e, half=half, c_=c_: e.scalar_tensor_tensor(
                    out=v3(hn[i][0:rows, half * 512:(half + 1) * 512]), in0=Hh[half], scalar=1.0 / 1024.0, in1=Hh[half],
                    op0=ALU.mult, op1=ALU.mult, accum_out=stat[0:rows, c_:c_ + 1]),
                   reads=[hk], writes=["hn%d" % i, "stat%d" % c_])
            cC = stat_col()
            op("dve", lambda e: e.scalar_tensor_tensor(out=stat[0:rows, cC:cC + 1], in0=stat[0:rows, cA:cA + 1], scalar=EPS,
                                                       in1=stat[0:rows, cB:cB + 1], op0=ALU.add, op1=ALU.add),
               reads=["stat%d" % cA, "stat%d" % cB], writes=["stat%d" % cC])
            cD = stat_col()
            op("pool", lambda e: e.tensor_tensor(out=stat[0:rows, cD:cD + 1], in0=stat[0:rows, cC:cC + 1], in1=cst[0:rows, 1:2], op=ALU.pow),
               reads=["stat%d" % cC, "cst"], writes=["stat%d" % cD])
            r2_ap, r2_k = stat[0:rows, cD:cD + 1], "stat%d" % cD
            op("dve", lambda e: e.tensor_scalar_mul(out=v3(hn[i][0:rows, :]), in0=H[0:rows], scalar1=r2_ap),
               reads=[hk, r2_k], writes=["hn%d" % i])

        def tr(u, i):
            rows = units[u][i][0]
            tb = ptb[i]
            tk = "pb%d" % (6 + i)
            for kt in range(8):
                op("pe", lambda e, kt=kt: e.transpose(tb[:, kt * rows:(kt + 1) * rows], hn[i][0:rows, kt * 128:(kt + 1) * 128],
                                                      identb[0:rows, 0:rows]),
                   reads=["hn%d" % i, "identb"], writes=[tk], sig=(kt == 7))
            op("dve", lambda e: e.tensor_copy(hnT[:, :, i * 128:i * 128 + rows], tb[:, 0:8 * rows].rearrange("p (k c) -> p k c", c=rows)),
               reads=[tk], writes=["hnT"])

        def post(u, i):
            rows, col0, xsrc, ydst = units[u][i]
            hk, H = slot_of[(u, i)]
            Hh = [H[0:rows, 0:4, :], H[0:rows, 4:8, :]]
            r_ap, r_k = rstd_from([(acc[2 * i + h_][0:rows, :], "pb%d" % (2 * i + h_)) for h_ in range(2)], rows, None)
            for half in range(2):
                cs_ = slice(half * 512, (half + 1) * 512)
                bk = "pb%d" % (2 * i + half)
                op("dve", lambda e, half=half, cs_=cs_: e.scalar_tensor_tensor(
                    out=acc[2 * i + half][0:rows, :], in0=acc[2 * i + half][0:rows, :], scalar=r_ap, in1=gpost[0:rows, 1, cs_],
                    op0=ALU.mult, op1=ALU.mult), reads=[bk, r_k, "gpost"], writes=[bk])
                op("dve", lambda e, half=half: e.tensor_add(Hh[half], v3(acc[2 * i + half][0:rows, :]), Hh[half]),
                   reads=[bk, hk], writes=[hk])
            dma("sp", ydst.rearrange("r (a b) -> r a b", b=128), H[0:rows], reads=[hk], is_out=True)
            free_slots.append((hk, H))

        def mlp(u, hooks, tail_hook=None):
            unit = units[u]
            ntile = len(unit)
            ncol = 128 * (ntile - 1) + unit[-1][0]

            def down(ft):
                for i, (rows, col0, xsrc, ydst) in enumerate(unit):
                    for half in range(2):
                        op("pe", lambda e, i=i, half=half, rows=rows: e.matmul(
                            acc[2 * i + half][0:rows, :], ffT[ft % 4][:, i * 128:i * 128 + rows], w_dn_sb[:, ft, half * 512:(half + 1) * 512],
                            start=(ft == 0), stop=(ft == 31)), reads=["ffT%d" % (ft % 4), "w_dn%d" % (ft // 4)],
                           writes=["pb%d" % (2 * i + half)], sig=(ft == 31 or (i == ntile - 1 and half == 1)))
            DL = 3
            for ft in range(32):
                if u == 0 and ft % 4 == 0:
                    w_scale(ft // 4)
                u_ap = upb[ft % 2][:, 0:ncol]
                uk = "pb%d" % (4 + ft % 2)
                for kt in range(8):
                    op("pe", lambda e, kt=kt: e.matmul(u_ap, w_up_sb[:, kt, ft * 128:(ft + 1) * 128], hnT[:, kt, 0:ncol],
                                                      start=(kt == 0), stop=(kt == 7)),
                       reads=["w_up%d" % (ft // 4), "hnT"], writes=[uk], sig=(kt == 7))
                op("act", lambda e: e.activation(out=u_ap, in_=u_ap, func=AF.Relu), reads=[uk], writes=[uk])
                op("act", lambda e: e.activation(out=ffT[ft % 4][:, 0:ncol], in_=u_ap, func=AF.Square), reads=[uk], writes=["ffT%d" % (ft % 4)])
                if ft >= DL:
                    down(ft - DL)
                if ft in hooks:
                    hooks[ft]()
            if tail_hook is not None:
                tail_hook()
            for ft in range(32 - DL, 32):
                down(ft)

        for i in range(len(units[0])):
            pre(0, i)
        for i in range(len(units[0])):
            tr(0, i)
        free_slots.append(ycat_slot(0))
        for u in range(NU):
            hooks = {}
            if u + 1 < NU:
                nt = len(units[u + 1])
                hooks[1] = (lambda u=u: pre(u + 1, 0))
                if nt > 1:
                    hooks[16] = (lambda u=u: pre(u + 1, 1))
            th_ = None
            if u + 1 < NU:
                th_ = (lambda u=u: [tr(u + 1, i) for i in range(len(units[u + 1]))])
            mlp(u, hooks, th_)
            if u + 1 < NU:
                if u + 1 < 8:
                    free_slots.append(ycat_slot(u + 1))
            for i in range(len(units[u])):
                post(u, i)
      except _Stop:
        pass
      kb.halted = False
      kb.finish()
    return nc


_IN_ORDER = ["x_prompt", "x_sample", "state_s5_re", "state_s5_im", "state_pool", "norm_mix_pre", "norm_mix_post",
             "norm_mlp_pre", "norm_mlp_post", "w_in", "s5_lambda_re", "s5_lambda_im", "s5_log_dt", "s5_b_re", "s5_b_im",
             "s5_c_re", "s5_c_im", "s5_d", "s5_w_glu", "pool_w", "pool_scale", "w_out", "w_mlp_up", "w_mlp_down"]


def make_in_maps(inp):
    f = lambda a: np.ascontiguousarray(np.asarray(a, dtype=np.float32))
    shared = {
        "g_mix_pre": f(inp["norm_mix_pre"]), "g_mix_post": f(inp["norm_mix_post"]),
        "g_mlp_pre": f(inp["norm_mlp_pre"]), "g_mlp_post": f(inp["norm_mlp_post"]),
        "w_in": f(inp["w_in"]), "lam_re": f(inp["s5_lambda_re"]).reshape(2048), "lam_im": f(inp["s5_lambda_im"]).reshape(2048),
        "log_dt": f(inp["s5_log_dt"]), "b_re": f(inp["s5_b_re"]).reshape(2048, 16), "b_im": f(inp["s5_b_im"]).reshape(2048, 16),
        "c_re": f(inp["s5_c_re"]).reshape(512, 64), "c_im": f(inp["s5_c_im"]).reshape(512, 64), "s5_d": f(inp["s5_d"]),
        "w_glu": f(inp["s5_w_glu"]), "pool_w": f(inp["pool_w"]), "pool_scale": f(inp["pool_scale"]),
        "w_out": f(inp["w_out"]), "w_up": f(inp["w_mlp_up"]), "w_dn": f(inp["w_mlp_down"]),
    }
    xp = f(inp["x_prompt"])
    xs = f(inp["x_sample"]).reshape(128, D)
    sre = f(inp["state_s5_re"]).reshape(128, 2048)
    sim = f(inp["state_s5_im"]).reshape(128, 2048)
    spl = f(inp["state_pool"])
    maps = []
    for c in range(NCORES):
        m = dict(shared)
        m["xp"] = xp[c]
        m["xs"] = xs[c * NSMP:(c + 1) * NSMP]
        m["st_re"] = sre[c * NSMP:(c + 1) * NSMP]
        m["st_im"] = sim[c * NSMP:(c + 1) * NSMP]
        m["st_pool"] = spl[c * NSMP:(c + 1) * NSMP]
        maps.append(m)
    return maps


def run(inp, debug=None, stop=None):
    nc = build_program(debug, stop)
    res = run_bass_kernel_spmd(nc, make_in_maps(inp), core_ids=list(range(NCORES)))
    return res.results


def kernel(**inputs):
    r = run(inputs)
    cat = lambda k: np.stack([np.asarray(r[c][k]) for c in range(NCORES)], axis=0)
    y_p = cat("y_p").reshape(8, SEQ, D).astype(np.float32)
    y_s = np.concatenate([np.asarray(r[c]["y_s"]) for c in range(NCORES)], axis=0).reshape(128, 1, D).astype(np.float32)
    re_p = cat("o_re_p").reshape(8, 32, 64).astype(np.float32)
    im_p = cat("o_im_p").reshape(8, 32, 64).astype(np.float32)
    pool_p = cat("o_pool_p").reshape(8, 15, 512).astype(np.float32)
    re_s = np.concatenate([np.asarray(r[c]["o_re_s"]) for c in range(NCORES)], axis=0).reshape(128, 32, 64).astype(np.float32)
    im_s = np.concatenate([np.asarray(r[c]["o_im_s"]) for c in range(NCORES)], axis=0).reshape(128, 32, 64).astype(np.float32)
    pool_s = np.concatenate([np.asarray(r[c]["o_pool_s"]) for c in range(NCORES)], axis=0).reshape(128, 15, 512).astype(np.float32)
    return (y_p, y_s, re_p, im_p, pool_p, re_s, im_s, pool_s)
```

```python
import math
from contextlib import ExitStack

import numpy as np
import concourse.bass as bass
import concourse.mybir as mybir
from concourse.bass_utils import run_bass_kernel_spmd

F32 = mybir.dt.float32
BF16 = mybir.dt.bfloat16
AF = mybir.ActivationFunctionType
ALU = mybir.AluOpType
AX = mybir.AxisListType

D = 1024
SEQ = 2048
NSMP = 16
NT = SEQ + NSMP
DFF = 4096
EPS = 1e-6
POOL_W = (2, 4, 8, 16)
NCORES = 8


class Tok:
    __slots__ = ("sem", "val", "eng")

    def __init__(self, sem, val, eng):
        self.sem, self.val, self.eng = sem, val, eng


class KB:
    def __init__(self, nc, es):
        self.nc = nc
        self.es = es
        self.eng = {"pe": nc.tensor, "act": nc.scalar, "dve": nc.vector, "pool": nc.gpsimd, "sp": nc.sync}
        self.sem = {}
        self.cnt = {}
        for e in ("pe", "act", "dve", "pool"):
            self.sem[e] = es.enter_context(nc.semaphore("sem_" + e))
            self.cnt[e] = 0
        self.pending = {e: [] for e in ("pe", "act", "dve", "pool")}
        self.waited = {}
        self.buf = {}
        self.dsems = {}
        self.dcnt = {}
        self.out_tokens = []
        self.halted = False

    def _deps(self, reads, writes):
        deps = []
        for k in reads:
            st = self.buf.get(k)
            if st and st["w"] is not None:
                deps.append(st["w"])
        for k in writes:
            st = self.buf.get(k)
            if st:
                if st["w"] is not None:
                    deps.append(st["w"])
                deps.extend(st["r"])
        return deps

    def _update(self, tok, reads, writes):
        for k in reads:
            st = self.buf.setdefault(k, {"w": None, "r": []})
            st["r"].append(tok)
            if len(st["r"]) > 64:
                st["r"] = self._prune(st["r"])
        for k in writes:
            self.buf[k] = {"w": tok, "r": []}

    @staticmethod
    def _prune(toks):
        best = {}
        for t in toks:
            key = id(t.sem)
            if t.val is None or key not in best or (best[key].val is not None and t.val > best[key].val):
                if t.val is None:
                    best[(key, id(t))] = t
                else:
                    best[key] = t
        return list(best.values())

    def _wait(self, e, deps):
        if self.halted:
            return
        eng = self.eng[e]
        need = {}
        for t in deps:
            if t.eng == "pe" and e == "pe":
                continue
            if t.val is None:
                raise RuntimeError("dependency on an unsignalled instruction (engine %s)" % t.eng)
            key = (e, id(t.sem))
            if self.waited.get(key, 0) >= t.val:
                continue
            if key not in need or need[key].val < t.val:
                need[key] = t
        for key, t in need.items():
            eng.wait_ge(t.sem, t.val)
            self.waited[key] = t.val

    def op(self, e, fn, reads=(), writes=(), sig=True, deps=()):
        if self.halted:
            return Tok(self.sem[e], 0, e)
        writes = list(writes) + [k for k in reads if k.startswith("pb") or k.startswith("ptb")]
        d = self._deps(reads, writes) + list(deps)
        self._wait(e, d)
        inst = fn(self.eng[e])
        if sig or e != "pe":
            self.cnt[e] += 1
            inst.then_inc(self.sem[e], 1)
            tok = Tok(self.sem[e], self.cnt[e], e)
            for p in self.pending[e]:
                p.val = self.cnt[e]
            self.pending[e] = []
        else:
            tok = Tok(self.sem[e], None, e)
            self.pending[e].append(tok)
        self._update(tok, reads, writes)
        return tok

    def dma(self, q, out, in_, reads=(), writes=(), sem=None, is_out=False, **kw):
        if self.halted:
            return Tok(self.sem["pe"], 0, "dma")
        d = self._deps(reads, writes)
        self._wait(q, d)
        name = (writes[0] if writes else ("o_" + reads[0] if reads else "out"))
        if name not in self.dsems:
            self.dsems[name] = self.es.enter_context(self.nc.semaphore("dsem_" + name))
            self.dcnt[name] = 0
        self.dcnt[name] += 16
        self.eng[q].dma_start(out=out, in_=in_, **kw).then_inc(self.dsems[name], 16)
        tok = Tok(self.dsems[name], self.dcnt[name], "dma")
        self._update(tok, reads, writes)
        if is_out:
            self.out_tokens.append(tok)
        return tok

    def finish(self):
        self._wait("sp", self.out_tokens)


class _Stop(Exception):
    pass


def build_program(debug=None, stop=None):
    nc = bass.Bass("TRN2", target_bir_lowering=False)
    dt_in = lambda name, shape: nc.dram_tensor(name, shape, F32, kind="ExternalInput").ap()
    dt_out = lambda name, shape: nc.dram_tensor(name, shape, F32, kind="ExternalOutput").ap()

    xp = dt_in("xp", [SEQ, D])
    xs = dt_in("xs", [NSMP, D])
    st_re = dt_in("st_re", [NSMP, 2048])
    st_im = dt_in("st_im", [NSMP, 2048])
    st_pool = dt_in("st_pool", [NSMP, 15, 512])
    g_mix_pre = dt_in("g_mix_pre", [D])
    g_mix_post = dt_in("g_mix_post", [D])
    g_mlp_pre = dt_in("g_mlp_pre", [D])
    g_mlp_post = dt_in("g_mlp_post", [D])
    w_in = dt_in("w_in", [D, D])
    lam_re = dt_in("lam_re", [2048])
    lam_im = dt_in("lam_im", [2048])
    log_dt = dt_in("log_dt", [32])
    b_re = dt_in("b_re", [2048, 16])
    b_im = dt_in("b_im", [2048, 16])
    c_re = dt_in("c_re", [512, 64])
    c_im = dt_in("c_im", [512, 64])
    s5_d = dt_in("s5_d", [512])
    w_glu = dt_in("w_glu", [32, 16, 16])
    pool_w = dt_in("pool_w", [4, 128, 128])
    pool_scale = dt_in("pool_scale", [512])
    w_out = dt_in("w_out", [D, D])
    w_up = dt_in("w_up", [D, DFF])
    w_dn = dt_in("w_dn", [DFF, D])

    y_p = dt_out("y_p", [SEQ, D])
    y_s = dt_out("y_s", [NSMP, D])
    o_re_p = dt_out("o_re_p", [2048])
    o_im_p = dt_out("o_im_p", [2048])
    o_pool_p = dt_out("o_pool_p", [15, 512])
    o_re_s = dt_out("o_re_s", [NSMP, 2048])
    o_im_s = dt_out("o_im_s", [NSMP, 2048])
    o_pool_s = dt_out("o_pool_s", [NSMP, 15, 512])
    dbg = None
    if debug:
        dbg = {k: nc.dram_tensor("dbg_" + k, list(shp), BF16, kind="ExternalOutput").ap() for k, shp in debug.items()}

    with ExitStack() as es:
      kb = KB(nc, es)
      try:
        op, dma = kb.op, kb.dma
        nm = [0]

        def sb(shape, dt, name=None, scope=es):
            nm[0] += 1
            return scope.enter_context(nc.sbuf_tensor(name or ("t%d" % nm[0]), list(shape), dt))

        def ps(shape, dt, name=None, scope=es):
            nm[0] += 1
            return scope.enter_context(nc.psum_tensor(name or ("p%d" % nm[0]), list(shape), dt))

        es.enter_context(nc.allow_non_contiguous_dma(reason="small strided parameter / state transfers"))

        ycat = sb([128, 8, NT], BF16, "ycat")
        identb = sb([128, 128], BF16, "identb")
        cst = sb([128, 8], F32, "cst")
        stat = sb([128, 64], F32, "stat")
        junk = sb([128, 512], BF16, "junk")
        pb = [ps([128, 512], F32, "pb%d" % i) for i in range(8)]
        ptb = [pb[6][:].bitcast(BF16), pb[7][:].bitcast(BF16)]

        p1 = es.enter_context(ExitStack())
        ident32 = sb([128, 128], F32, "ident32", p1)
        op("pool", lambda e: e.memset(ident32[:], 1.0), writes=["ident32"])
        op("pool", lambda e: e.affine_select(out=ident32[:], in_=ident32[:], pattern=[[-1, 128]],
                                             compare_op=ALU.is_equal, fill=0.0, base=0, channel_multiplier=1),
           reads=["ident32"], writes=["ident32"])
        op("dve", lambda e: e.tensor_copy(identb[:], ident32[:]), reads=["ident32"], writes=["identb"])
        for i, v in enumerate([EPS, -0.5, math.pi, 2 * math.pi, -math.pi, 1.5 * math.pi]):
            op("pool", lambda e, i=i, v=v: e.memset(cst[:, i:i + 1], v), writes=["cst"])

        def barrier():
            toks = [op(e_, lambda e, c_=c_: e.memset(cst[:, c_:c_ + 1], 0.0), writes=["bar_" + e_]) for e_, c_ in (("dve", 6), ("pool", 7))]
            toks.append(op("act", lambda e: e.activation(out=junk[:, 0:1], in_=cst[:, 0:1], func=AF.Copy), reads=["cst"], writes=["junk"]))
            toks.append(op("pe", lambda e: e.matmul(pb[5][0:16, 0:16], identb[:, 0:16], identb[:, 0:16], start=True, stop=True),
                           reads=["identb"], writes=["pb5"]))
            if kb.halted:
                return []
            toks += [Tok(kb.dsems[k_], kb.dcnt[k_], "dma") for k_ in kb.dsems]
            for e_ in ("dve", "pool", "act", "pe", "sp"):
                kb._wait(e_, toks)
            return toks

        stat_i = [0]

        def stat_col():
            stat_i[0] = (stat_i[0] + 1) % 60
            return stat_i[0]

        def rstd_from(srcs, rows, key_reads):
            cols = []
            srcs2 = []
            for ap_, key in srcs:
                n_ = ap_.shape[-1] if len(ap_.shape) == 2 else 1
                if n_ > 512:
                    for o_ in range(0, n_, 512):
                        srcs2.append((ap_[:, o_:o_ + 512], key))
                else:
                    srcs2.append((ap_, key))
            assert len(srcs2) <= 2
            for ap_, key in srcs2:
                c = stat_col()
                n = 1
                for v_ in ap_.shape[1:]:
                    n *= v_
                jv = junk[0:rows, 0:n]
                if len(ap_.shape) == 3:
                    jv = jv.rearrange("p (a b) -> p a b", b=ap_.shape[2])
                op("act", lambda e, ap_=ap_, c=c, n=n, jv=jv: e.activation(out=jv, in_=ap_, func=AF.Square,
                                                                 scale=1.0 / 32.0, accum_out=stat[0:rows, c:c + 1]),
                   reads=[key], writes=["junk", "stat%d" % c])
                cols.append(c)
            c2 = stat_col()
            if len(cols) == 2:
                op("dve", lambda e: e.scalar_tensor_tensor(out=stat[0:rows, c2:c2 + 1], in0=stat[0:rows, cols[0]:cols[0] + 1],
                                                           scalar=EPS, in1=stat[0:rows, cols[1]:cols[1] + 1],
                                                           op0=ALU.add, op1=ALU.add),
                   reads=["stat%d" % cols[0], "stat%d" % cols[1]], writes=["stat%d" % c2])
            else:
                op("dve", lambda e: e.tensor_scalar_add(out=stat[0:rows, c2:c2 + 1], in0=stat[0:rows, cols[0]:cols[0] + 1],
                                                        scalar1=EPS),
                   reads=["stat%d" % cols[0]], writes=["stat%d" % c2])
            c3 = stat_col()
            op("pool", lambda e: e.tensor_tensor(out=stat[0:rows, c3:c3 + 1], in0=stat[0:rows, c2:c2 + 1],
                                                 in1=cst[0:rows, 1:2], op=ALU.pow),
               reads=["stat%d" % c2, "cst"], writes=["stat%d" % c3])
            return stat[0:rows, c3:c3 + 1], "stat%d" % c3

        if True:
            w_in_sb = sb([128, 8, D], BF16, "w_in_sb", p1)
            g0 = sb([128, D], F32, "g0", p1)
            dma("sp", g0[:], g_mix_pre.partition_broadcast(128), writes=["g0"], sem="setup")

            toep_sb = sb([128, 32, 128], BF16, "toep_sb", p1)
            bpow_sb = sb([128, 32, 2, 64], BF16, "bpow_sb", p1)
            cpow_sb = sb([128, 16, 2, 128], BF16, "cpow_sb", p1)
            glu_sb = sb([128, 4, 128], BF16, "glu_sb", p1)
            poolw_sb = sb([128, 4, 128], BF16, "poolw_sb", p1)
            pscale = sb([128, 4], F32, "pscale", p1)
            invcnt = sb([128, 4, 16], F32, "invcnt", p1)
            ER = sb([128, 16, 128], F32, "ER", p1)
            EI = sb([128, 16, 128], F32, "EI", p1)
            rho8 = sb([128, 16], F32, "rho8", p1)
            P1r = sb([128, 16], F32, "P1r", p1)
            P1i = sb([128, 16], F32, "P1i", p1)
            carry = sb([128, 2, 16], F32, "carry", p1)

            with ExitStack() as su:
                lamr = sb([128, 16], F32, "lamr", su)
                lami = sb([128, 16], F32, "lami", su)
                ldt = sb([128, 16], F32, "ldt", su)
                Bq = [sb([128, 16, 16], F32, "Bq%d" % i, su) for i in range(2)]
                Cnat = [sb([128, 4, 64], F32, "Cnat%d" % i, su) for i in range(2)]
                Cq = [sb([128, 16, 16], F32, "Cq%d" % i, su) for i in range(2)]
                Dcol = sb([128, 32], F32, "Dcol", su)
                lam_nat = [sb([16, 128], F32, "lam_nat%d" % i, su) for i in range(2)]
                ldt_row = sb([1, 32], F32, "ldt_row", su)
                ones_row = sb([1, 128], F32, "ones_row", su)
                Dg = sb([32, 16], F32, "Dg", su)
                Dg8 = sb([32, 8, 16], F32, "Dg8", su)
                Wgl = sb([128, 4, 16], F32, "Wgl", su)
                psc_nat = sb([4, 128], F32, "psc_nat", su)
                bmask = sb([128, 8, 16], F32, "bmask", su)
                su_b = ExitStack()
                Bnat = [sb([16, 2048], F32, "Bnat%d" % i, su_b) for i in range(2)]
                dma("sp", lam_nat[0][:], lam_re.rearrange("(j q) -> j q", q=128), writes=["lam_nat0"])
                dma("sp", lam_nat[1][:], lam_im.rearrange("(j q) -> j q", q=128), writes=["lam_nat1"])
                dma("sp", ldt_row[:], log_dt.rearrange("(o g) -> o g", o=1), writes=["ldt_row"])
                dma("sp", Bnat[0][:], b_re.rearrange("(j q) h -> j (q h)", q=128), writes=["Bnat0"])
                dma("sp", Bnat[1][:], b_im.rearrange("(j q) h -> j (q h)", q=128), writes=["Bnat1"])
                dma("sp", Cnat[0][:], c_re.rearrange("(T r) p -> r T p", r=128), writes=["Cnat0"])
                dma("sp", Cnat[1][:], c_im.rearrange("(T r) p -> r T p", r=128), writes=["Cnat1"])
                dma("sp", Dg[:], s5_d.rearrange("(g h) -> g h", h=16), writes=["Dg"])
                dma("sp", Wgl[:], w_glu.rearrange("(ct g8) h k -> (g8 h) ct k", g8=8), writes=["Wgl"])
                dma("sp", psc_nat[:], pool_scale.rearrange("(g d) -> g d", d=128), writes=["psc_nat"])
                op("pool", lambda e: e.memset(ones_row[:], 1.0), writes=["ones_row"])
                for part in range(2):
                    op("pe", lambda e, part=part: e.matmul(pb[0][:, 16 * part:16 * part + 16], lam_nat[part][:], ident32[0:16, 0:16], start=True, stop=True),
                       reads=["lam_nat%d" % part, "ident32"], writes=["pb0"], sig=(part == 1))
                op("pe", lambda e: e.matmul(pb[0][:, 32:64], ones_row[:], ldt_row[:], start=True, stop=True),
                   reads=["ones_row", "ldt_row"], writes=["pb0"])
                op("pe", lambda e: e.matmul(pb[0][:, 64:68], psc_nat[:], ident32[0:4, 0:4], start=True, stop=True),
                   reads=["psc_nat", "ident32"], writes=["pb0"])
                op("dve", lambda e: e.tensor_copy(Dg8[:], Dg[:].unsqueeze(1).to_broadcast([32, 8, 16])), reads=["Dg"], writes=["Dg8"])
                op("pe", lambda e: e.matmul(pb[0][:, 96:128], Dg8[:].rearrange("p s h -> p (s h)"), ident32[0:32, 0:32], start=True, stop=True),
                   reads=["Dg8", "ident32"], writes=["pb0"])
                op("dve", lambda e: e.tensor_copy(lamr[:], pb[0][:, 0:16]), reads=["pb0"], writes=["lamr"])
                op("dve", lambda e: e.tensor_copy(lami[:], pb[0][:, 16:32]), reads=["pb0"], writes=["lami"])
                op("dve", lambda e: e.tensor_copy(ldt[0:64, :], pb[0][0:64, 32:64:2]), reads=["pb0"], writes=["ldt"])
                op("dve", lambda e: e.tensor_copy(ldt[64:128, :], pb[0][64:128, 33:64:2]), reads=["pb0"], writes=["ldt"])
                op("dve", lambda e: e.tensor_copy(pscale[:], pb[0][:, 64:68]), reads=["pb0"], writes=["pscale"])
                op("dve", lambda e: e.tensor_copy(Dcol[:], pb[0][:, 96:128]), reads=["pb0"], writes=["Dcol"])
                for part in range(2):
                    for h_ in range(16):
                        op("pe", lambda e, part=part, h_=h_: e.matmul(pb[1][:, (part * 16 + h_) * 16:(part * 16 + h_ + 1) * 16],
                                                                        Bnat[part][:, h_:2048:16], ident32[0:16, 0:16], start=True, stop=True),
                           reads=["Bnat%d" % part, "ident32"], writes=["pb1"], sig=(h_ == 15))
                    bq_tok = op("dve", lambda e, part=part: e.tensor_copy(Bq[part][:].rearrange("p j h -> p h j"),
                                                                 pb[1][:, part * 256:(part + 1) * 256].rearrange("p (h j) -> p h j", j=16)),
                       reads=["pb1"], writes=["Bq%d" % part])
                op("pool", lambda e: e.memset(bmask[:], 1.0), writes=["bmask"])
                op("pool", lambda e: e.affine_select(out=bmask[:], in_=bmask[:], pattern=[[16, 8], [0, 16]], compare_op=ALU.is_ge, fill=0.0,
                                                     base=15, channel_multiplier=-1), reads=["bmask"], writes=["bmask"])
                op("pool", lambda e: e.affine_select(out=bmask[:], in_=bmask[:], pattern=[[-16, 8], [0, 16]], compare_op=ALU.is_ge, fill=0.0,
                                                     base=0, channel_multiplier=1), reads=["bmask"], writes=["bmask"])
                op("dve", lambda e: e.tensor_tensor(out=glu_sb[:].rearrange("p c (g k) -> p c g k", k=16),
                                                    in0=Wgl[:].unsqueeze(2).to_broadcast([128, 4, 8, 16]),
                                                    in1=bmask[:].unsqueeze(1).to_broadcast([128, 4, 8, 16]), op=ALU.mult),
                   reads=["Wgl", "bmask"], writes=["glu_sb"])
                su_b.close()
                for e_ in ("pool", "act", "pe", "sp"):
                    kb._wait(e_, [bq_tok])
                for gi, w in enumerate(POOL_W):
                    for t in range(16):
                        op("pool", lambda e, gi=gi, t=t, w=w: e.memset(invcnt[:, gi, t:t + 1], 1.0 / min(t + 1, w)),
                           writes=["invcnt"])

                dtq = sb([128, 16], F32, "dtq", su)
                aq = sb([128, 16], F32, "aq", su)
                thq = sb([128, 16], F32, "thq", su)
                tmpq = [sb([128, 16], F32, "tmpq%d" % i, su) for i in range(6)]
                twopi = sb([128, 16], F32, "twopi", su)
                op("pool", lambda e: e.memset(twopi[:], 2 * math.pi), writes=["twopi"])
                def series(out_t, ok, x_t, xk, n, xscale, tmp_t, tk):
                    op("dve", lambda e: e.memset(out_t[:], 1.0), writes=[ok])
                    for k in range(n, 0, -1):
                        op("dve", lambda e, k=k: e.scalar_tensor_tensor(out=tmp_t[:], in0=out_t[:], scalar=xscale / k, in1=x_t[:],
                                                                        op0=ALU.mult, op1=ALU.mult), reads=[ok, xk], writes=[tk])
                        op("dve", lambda e: e.tensor_scalar_add(out=out_t[:], in0=tmp_t[:], scalar1=1.0), reads=[tk], writes=[ok])
                dma("pool", w_in_sb[:, 0:4, :], w_in.rearrange("(kt p) n -> p kt n", p=128)[:, 0:4, :], writes=["w_in_sb"], sem="w_in")
                dma("pool", w_in_sb[:, 4:8, :], w_in.rearrange("(kt p) n -> p kt n", p=128)[:, 4:8, :], writes=["w_in_sb"], sem="w_in")
                dma("pool", poolw_sb[:], pool_w.rearrange("g c d -> c g d"), writes=["poolw_sb"])
                series(dtq, "dtq", ldt, "ldt", 12, 0.125, tmpq[0], "tmpq0")
                for _ in range(3):
                    op("dve", lambda e: e.tensor_mul(dtq[:], dtq[:], dtq[:]), reads=["dtq"], writes=["dtq"])
                op("dve", lambda e: e.tensor_mul(aq[:], lamr[:], dtq[:]), reads=["lamr", "dtq"], writes=["aq"])
                op("dve", lambda e: e.tensor_mul(thq[:], lami[:], dtq[:]), reads=["lami", "dtq"], writes=["thq"])
                PR = sb([128, 9, 16], F32, "PR", su)
                PI = sb([128, 9, 16], F32, "PI", su)
                VR = sb([128, 9, 16], F32, "VR", su)
                VI = sb([128, 9, 16], F32, "VI", su)
                mag = sb([128, 16], F32, "mag", su)
                imag = sb([128, 16], F32, "imag", su)
                em1 = sb([128, 16], F32, "em1", su)
                cs = sb([128, 2, 16], F32, "cs", su)
                op("dve", lambda e: e.memset(em1[:], 1.0), writes=["em1"])
                for k in range(8, 1, -1):
                    op("dve", lambda e, k=k: e.scalar_tensor_tensor(out=tmpq[0][:], in0=em1[:], scalar=1.0 / k, in1=aq[:],
                                                                    op0=ALU.mult, op1=ALU.mult), reads=["em1", "aq"], writes=["tmpq0"])
                    op("dve", lambda e: e.tensor_scalar_add(out=em1[:], in0=tmpq[0][:], scalar1=1.0), reads=["tmpq0"], writes=["em1"])
                op("dve", lambda e: e.tensor_mul(em1[:], em1[:], aq[:]), reads=["em1", "aq"], writes=["em1"])
                op("dve", lambda e: e.tensor_scalar_add(out=mag[:], in0=em1[:], scalar1=1.0), reads=["em1"], writes=["mag"])
                m8 = tmpq[5]
                op("dve", lambda e: e.tensor_mul(imag[:], mag[:], mag[:]), reads=["mag"], writes=["imag"])
                op("dve", lambda e: e.tensor_mul(rho8[:], imag[:], imag[:]), reads=["imag"], writes=["rho8"])
                op("dve", lambda e: e.tensor_mul(rho8[:], rho8[:], rho8[:]), reads=["rho8"], writes=["rho8"])
                op("dve", lambda e: e.reciprocal(imag[:], imag[:]), reads=["imag"], writes=["imag"])
                op("dve", lambda e: e.reciprocal(m8[:], rho8[:]), reads=["rho8"], writes=["tmpq5"])
                qi = sb([128, 16], mybir.dt.int32, "qi", su)
                for idx, shift in ((0, 0.5 * math.pi), (1, 0.0)):
                    t0 = tmpq[idx]
                    t1_ = tmpq[2 + idx]
                    k0 = "tmpq%d" % idx
                    k1 = "tmpq%d" % (2 + idx)
                    op("dve", lambda e, t0=t0, shift=shift: e.tensor_scalar_add(out=t0[:], in0=thq[:], scalar1=shift), reads=["thq"], writes=[k0])
                    op("dve", lambda e, t0=t0, t1_=t1_: e.tensor_scalar_mul(out=t1_[:], in0=t0[:], scalar1=1.0 / (2 * math.pi)), reads=[k0], writes=[k1])
                    op("dve", lambda e, t1_=t1_: e.tensor_copy(qi[:], t1_[:]), reads=[k1], writes=["qi"])
                    op("dve", lambda e, t1_=t1_: e.tensor_copy(t1_[:], qi[:]), reads=["qi"], writes=[k1])
                    op("dve", lambda e, t0=t0, t1_=t1_: e.scalar_tensor_tensor(out=t0[:], in0=t1_[:], scalar=-2 * math.pi, in1=t0[:],
                                                                               op0=ALU.mult, op1=ALU.add), reads=[k0, k1], writes=[k0])
                    op("dve", lambda e, t0=t0, t1_=t1_: e.tensor_scalar(out=t1_[:], in0=t0[:], scalar1=math.pi, scalar2=2 * math.pi,
                                                                        op0=ALU.is_gt, op1=ALU.mult), reads=[k0], writes=[k1])
                    op("dve", lambda e, t0=t0, t1_=t1_: e.tensor_sub(t0[:], t0[:], t1_[:]), reads=[k0, k1], writes=[k0])
                    op("dve", lambda e, t0=t0, t1_=t1_: e.tensor_scalar(out=t1_[:], in0=t0[:], scalar1=-math.pi, scalar2=2 * math.pi,
                                                                        op0=ALU.is_lt, op1=ALU.mult), reads=[k0], writes=[k1])
                    op("dve", lambda e, t0=t0, t1_=t1_: e.tensor_add(t0[:], t0[:], t1_[:]), reads=[k0, k1], writes=[k0])
                    op("act", lambda e, t0=t0, idx=idx: e.activation(out=cs[:, idx, :], in_=t0[:], func=AF.Sin), reads=[k0], writes=["cs"])
                op("dve", lambda e: e.memset(PR[:, 0, :], 1.0), writes=["PR"])
                op("dve", lambda e: e.memset(PI[:, 0, :], 0.0), writes=["PI"])
                op("dve", lambda e: e.tensor_mul(PR[:, 1, :], mag[:], cs[:, 0, :]), reads=["mag", "cs"], writes=["PR"])
                op("dve", lambda e: e.tensor_mul(PI[:, 1, :], mag[:], cs[:, 1, :]), reads=["mag", "cs"], writes=["PI"])
                op("dve", lambda e: e.tensor_copy(P1r[:], PR[:, 1, :]), reads=["PR"], writes=["P1r"])
                op("dve", lambda e: e.tensor_copy(P1i[:], PI[:, 1, :]), reads=["PI"], writes=["P1i"])
                op("dve", lambda e: e.memset(VR[:, 0, :], 1.0), writes=["VR"])
                op("dve", lambda e: e.memset(VI[:, 0, :], 0.0), writes=["VI"])
                op("dve", lambda e: e.tensor_mul(VR[:, 1, :], PR[:, 1, :], imag[:]), reads=["PR", "imag"], writes=["VR"])
                op("dve", lambda e: e.scalar_tensor_tensor(out=VI[:, 1, :], in0=PI[:, 1, :], scalar=-1.0, in1=imag[:],
                                                           op0=ALU.mult, op1=ALU.mult), reads=["PI", "imag"], writes=["VI"])

                tq = [sb([128, 1024], F32, "tq%d" % i, su) for i in range(2)]
                tqp = [sb([128, 2048], F32, "tqp%d" % i, su) for i in range(2)]

                def cmul(outr, outi, ar, ai, br, bi, shape, rk, wk, eng="dve"):
                    n = 1
                    for v in shape[1:]:
                        n *= v
                    tset, tkeys = (tq, ["tq0", "tq1"]) if eng == "dve" else (tqp, ["tqp0", "tqp1"])
                    t0 = tset[0][:, 0:n]
                    t1 = tset[1][:, 0:n]
                    if len(shape) == 3:
                        t0 = t0.rearrange("p (a b) -> p a b", b=shape[2])
                        t1 = t1.rearrange("p (a b) -> p a b", b=shape[2])
                    elif len(shape) == 4:
                        t0 = t0.rearrange("p (a b c) -> p a b c", b=shape[2], c=shape[3])
                        t1 = t1.rearrange("p (a b c) -> p a b c", b=shape[2], c=shape[3])
                    op(eng, lambda e: e.tensor_mul(t0, ar, br), reads=rk, writes=[tkeys[0]])
                    op(eng, lambda e: e.tensor_mul(t1, ai, bi), reads=rk, writes=[tkeys[1]])
                    op(eng, lambda e: e.tensor_sub(outr, t0, t1), reads=tkeys, writes=wk)
                    op(eng, lambda e: e.tensor_mul(t0, ar, bi), reads=rk, writes=[tkeys[0]])
                    op(eng, lambda e: e.tensor_mul(t1, ai, br), reads=rk, writes=[tkeys[1]])
                    op(eng, lambda e: e.tensor_add(outi, t0, t1), reads=tkeys, writes=wk)

                for (TR, TI, nmk) in ((PR, PI, ["PR", "PI"]), (VR, VI, ["VR", "VI"])):
                    for n in (1, 2, 4):
                        cmul(TR[:, n + 1:2 * n + 1, :], TI[:, n + 1:2 * n + 1, :], TR[:, 1:n + 1, :], TI[:, 1:n + 1, :],
                             TR[:, n:n + 1, :].to_broadcast([128, n, 16]), TI[:, n:n + 1, :].to_broadcast([128, n, 16]),
                             [128, n, 16], nmk, nmk)
                zr = tmpq[0]
                zi = tmpq[1]
                den = tmpq[2]
                t3 = tmpq[3]
                t4 = tmpq[4]
                op("dve", lambda e: e.tensor_mul(den[:], lamr[:], lamr[:]), reads=["lamr"], writes=["tmpq2"])
                op("dve", lambda e: e.tensor_mul(t3[:], lami[:], lami[:]), reads=["lami"], writes=["tmpq3"])
                op("dve", lambda e: e.tensor_add(den[:], den[:], t3[:]), reads=["tmpq2", "tmpq3"], writes=["tmpq2"])
                op("dve", lambda e: e.reciprocal(den[:], den[:]), reads=["tmpq2"], writes=["tmpq2"])
                op("dve", lambda e: e.tensor_scalar_add(out=t4[:], in0=cs[:, 0, :], scalar1=-1.0), reads=["cs"], writes=["tmpq4"])
                op("dve", lambda e: e.tensor_mul(t3[:], em1[:], cs[:, 0, :]), reads=["em1", "cs"], writes=["tmpq3"])
                op("dve", lambda e: e.tensor_add(t4[:], t4[:], t3[:]), reads=["tmpq4", "tmpq3"], writes=["tmpq4"])
                op("dve", lambda e: e.tensor_mul(zr[:], t4[:], lamr[:]), reads=["tmpq4", "lamr"], writes=["tmpq0"])
                op("dve", lambda e: e.tensor_mul(t3[:], PI[:, 1, :], lami[:]), reads=["PI", "lami"], writes=["tmpq3"])
                op("dve", lambda e: e.tensor_add(zr[:], zr[:], t3[:]), reads=["tmpq0", "tmpq3"], writes=["tmpq0"])
                op("dve", lambda e: e.tensor_mul(zr[:], zr[:], den[:]), reads=["tmpq0", "tmpq2"], writes=["tmpq0"])
                op("dve", lambda e: e.tensor_mul(zi[:], PI[:, 1, :], lamr[:]), reads=["PI", "lamr"], writes=["tmpq1"])
                op("dve", lambda e: e.tensor_mul(t3[:], t4[:], lami[:]), reads=["tmpq4", "lami"], writes=["tmpq3"])
                op("dve", lambda e: e.tensor_sub(zi[:], zi[:], t3[:]), reads=["tmpq1", "tmpq3"], writes=["tmpq1"])
                op("dve", lambda e: e.tensor_mul(zi[:], zi[:], den[:]), reads=["tmpq1", "tmpq2"], writes=["tmpq1"])
                Bb = [sb([128, 16, 16], F32, "Bb%d" % i, su) for i in range(2)]
                cmul(Bb[0][:], Bb[1][:], Bq[0][:], Bq[1][:], zr[:].unsqueeze(2).to_broadcast([128, 16, 16]),
                     zi[:].unsqueeze(2).to_broadcast([128, 16, 16]), [128, 16, 16], ["Bq0", "Bq1", "tmpq0", "tmpq1"], ["Bb0", "Bb1"])
                for part in range(2):
                    for T in range(4):
                        for gl in range(2):
                            rhs = ident32[:, :].rearrange("p (a b) -> p a b", b=32)[:, :, 16 * gl:16 * gl + 16]
                            last = (T == 3 and gl == 1)
                            op("pe", lambda e, part=part, T=T, gl=gl, rhs=rhs: e.matmul(
                                pb[part][gl * 64:(gl + 1) * 64, T * 64:(T + 1) * 64].rearrange("p (a b) -> p a b", b=16), Cnat[part][:, T, :], rhs, start=True, stop=True),
                               reads=["Cnat%d" % part, "ident32"], writes=["pb%d" % part], sig=last)
                    op("dve", lambda e, part=part: e.tensor_copy(Cq[part][:].rearrange("p j h -> p (j h)"), pb[part][:, 0:256]),
                       reads=["pb%d" % part], writes=["Cq%d" % part])
                Wq = [sb([128, 16, 8, 16], F32, "Wq%d" % i, su) for i in range(2)]
                Xq = [sb([128, 16, 8, 16], F32, "Xq%d" % i, su) for i in range(2)]
                BPq = [sb([128, 16, 8, 16], F32, "BPq%d" % i, su) for i in range(2)]
                mask32 = sb([128, 128], F32, "mask32", su)
                op("pool", lambda e: e.memset(mask32[:], 1.0), writes=["mask32"])
                op("pool", lambda e: e.affine_select(out=mask32[:].rearrange("p (t h) -> p t h", h=16),
                                                     in_=mask32[:].rearrange("p (t h) -> p t h", h=16),
                                                     pattern=[[16, 8], [0, 16]], compare_op=ALU.is_ge, fill=0.0, base=15,
                                                     channel_multiplier=-1), reads=["mask32"], writes=["mask32"])
                tmpT = [sb([128, 4, 128], F32, "tmpT%d" % i, su) for i in range(2)]
                BPb = [sb([128, 16, 128], BF16, "BPb%d" % i, su) for i in range(2)]
                HK = lambda nm, jh: ["%s0_h%d" % (nm, jh), "%s1_h%d" % (nm, jh)]
                for jh in range(2):
                    js = slice(jh * 8, jh * 8 + 8)
                    b4h = lambda t_: t_[:, js, :].unsqueeze(2).to_broadcast([128, 8, 8, 16])
                    pwh = lambda T_: T_[:, 1:9, js].rearrange("p s j -> p j s").unsqueeze(3).to_broadcast([128, 8, 8, 16])
                    cmul(Xq[0][:, js], Xq[1][:, js], b4h(Cq[0]), b4h(Cq[1]), pwh(PR), pwh(PI), [128, 8, 8, 16], ["Cq0", "Cq1", "PR", "PI"], HK("Xq", jh))
                    op("dve", lambda e: e.tensor_scalar_mul(out=Xq[1][:, js], in0=Xq[1][:, js], scalar1=-1.0),
                       reads=HK("Xq", jh), writes=[HK("Xq", jh)[1]])
                for jh in range(2):
                    js = slice(jh * 8, jh * 8 + 8)
                    b4h = lambda t_: t_[:, js, :].unsqueeze(2).to_broadcast([128, 8, 8, 16])
                    pwh = lambda T_: T_[:, 1:9, js].rearrange("p s j -> p j s").unsqueeze(3).to_broadcast([128, 8, 8, 16])
                    p8h = lambda T_: T_[:, 8, js].unsqueeze(2).unsqueeze(3).to_broadcast([128, 8, 8, 16])
                    cmul(Wq[0][:, js], Wq[1][:, js], b4h(Bb[0]), b4h(Bb[1]), pwh(VR), pwh(VI), [128, 8, 8, 16], ["Bb0", "Bb1", "VR", "VI"], HK("Wq", jh), eng="pool")
                    cmul(BPq[0][:, js], BPq[1][:, js], Wq[0][:, js], Wq[1][:, js], p8h(PR), p8h(PI), [128, 8, 8, 16], HK("Wq", jh) + ["PR", "PI"], HK("BPq", jh), eng="pool")
                for jh in range(2):
                    js = slice(jh * 8, jh * 8 + 8)
                    for part in range(2):
                        op("act", lambda e, part=part: e.activation(out=cpow_sb[:, js, part, :], in_=Xq[part][:, js].rearrange("p j t h -> p j (t h)"),
                                                                    func=AF.Copy), reads=[HK("Xq", jh)[part]], writes=["cpow_sb"])
                        op("act", lambda e, part=part: e.activation(out=BPb[part][:, js], in_=BPq[part][:, js].rearrange("p j s h -> p j (s h)"), func=AF.Copy),
                           reads=[HK("BPq", jh)[part]], writes=[HK("BPb", jh)[part]])
                    for jb in (2 * jh, 2 * jh + 1):
                        for gl in range(2):
                            bank = pb[4 + gl]
                            bk = "pb%d" % (4 + gl)
                            rows = slice(gl * 64, gl * 64 + 64)
                            for jq in range(4):
                                J = jb * 4 + jq
                                for part in range(2):
                                    op("pe", lambda e, J=J, rows=rows, jq=jq, part=part, bank=bank: e.matmul(
                                        bank[:, jq * 128 + part * 64:jq * 128 + part * 64 + 64], BPb[part][rows, J, :], identb[rows, rows],
                                        start=True, stop=True),
                                       reads=[HK("BPb", jh)[part], "identb"], writes=[bk], sig=(jq == 3 and part == 1))
                            gst = 2 * (jb * 4) + gl
                            op("act", lambda e, gst=gst, bank=bank: e.activation(out=bpow_sb[:, gst:gst + 7:2, :, :].rearrange("p g a b -> p g (a b)"),
                                                                             in_=bank[:].rearrange("p (g c) -> p g c", c=128), func=AF.Copy),
                               reads=[bk], writes=["bpow_sb"])
                    for jb in (2 * jh, 2 * jh + 1):
                        for gl in range(2):
                            bank = pb[2 + gl]
                            bk = "pb%d" % (2 + gl)
                            rows = slice(gl * 64, gl * 64 + 64)
                            for jq in range(4):
                                J = jb * 4 + jq
                                op("pe", lambda e, J=J, rows=rows, bank=bank, jq=jq: e.matmul(
                                    bank[:, jq * 128:(jq + 1) * 128], Wq[0][rows, J, :, :].rearrange("p s h -> p (s h)"),
                                    Xq[0][rows, J, :, :].rearrange("p t h -> p (t h)"), start=True, stop=False),
                                   reads=[HK("Wq", jh)[0], HK("Xq", jh)[0]], writes=[bk], sig=False)
                                op("pe", lambda e, J=J, rows=rows, bank=bank, jq=jq: e.matmul(
                                    bank[:, jq * 128:(jq + 1) * 128], Wq[1][rows, J, :, :].rearrange("p s h -> p (s h)"),
                                    Xq[1][rows, J, :, :].rearrange("p t h -> p (t h)"), start=False, stop=True),
                                   reads=[HK("Wq", jh)[1], HK("Xq", jh)[1]], writes=[bk], sig=(jq == 3))
                            tt_ = tmpT[gl]
                            tk = "tmpT%d" % gl
                            op("dve", lambda e, bank=bank, tt_=tt_: e.tensor_tensor(out=tt_[:], in0=bank[:].rearrange("p (g c) -> p g c", c=128),
                                                                                  in1=mask32[:].unsqueeze(1).to_broadcast([128, 4, 128]), op=ALU.mult),
                               reads=[bk, "mask32"], writes=[tk])
                            for jq in range(4):
                                g = 2 * (jb * 4 + jq) + gl
                                op("dve", lambda e, tt_=tt_, g=g, jq=jq: e.scalar_tensor_tensor(out=toep_sb[:, g, :], in0=ident32[:], scalar=Dcol[:, g:g + 1],
                                                                                              in1=tt_[:, jq, :], op0=ALU.mult, op1=ALU.add),
                                   reads=[tk, "ident32", "Dcol"], writes=["toep_sb"])
                op("dve", lambda e: e.tensor_mul(ER[:, :, 0], PR[:, 8, :], m8[:]), reads=["PR", "tmpq5"], writes=["ER"])
                op("dve", lambda e: e.scalar_tensor_tensor(out=EI[:, :, 0], in0=PI[:, 8, :], scalar=-1.0, in1=m8[:],
                                                           op0=ALU.mult, op1=ALU.mult), reads=["PI", "tmpq5"], writes=["EI"])
                for k in range(7):
                    n = 1 << k
                    cmul(ER[:, :, n:2 * n], EI[:, :, n:2 * n], ER[:, :, 0:n], EI[:, :, 0:n],
                         ER[:, :, n - 1:n].to_broadcast([128, 16, n]), EI[:, :, n - 1:n].to_broadcast([128, 16, n]),
                         [128, 16, n], ["ER", "EI"], ["ER", "EI"])
                barrier()
            op("dve", lambda e: e.memset(carry[:], 0.0), writes=["carry0", "carry1", "carry2", "carry3"])
            if stop == "setup":
                kb.halted = True

            pa = ExitStack()
            xa = [sb([128, D], F32, "xa%d" % i, pa) for i in range(3)]
            xn = [sb([128, D], BF16, "xn%d" % i, pa) for i in range(3)]
            arA = sb([128, 8192], BF16, "arA", pa)
            xnT = arA[:].rearrange("p (k s c) -> p k s c", k=8, s=8)
            ygU = arA[:, 0:4096].rearrange("p (t c) -> p t c", t=8)
            ygfm = arA[:, 4096:8192].rearrange("p (ct t c) -> p ct t c", ct=4, t=8)
            arB = sb([128, 4096], BF16, "arB", pa)
            U_sb = arB[:].rearrange("p (g s h) -> p g s h", g=32, s=8)
            Hsb = arB[:].rearrange("p (a j c) -> p a j c", a=2, j=16)
            ub = sb([128, 4, 16 + 1024], F32, "ub", pa)
            wa = sb([128, 1040], F32, "wa", pa)
            wb = sb([128, 1040], F32, "wb", pa)
            arC = sb([128, 1024], F32, "arC", pa)
            arD = sb([128, 1024], F32, "arD", pa)
            Gm = [arC[:, i * 512:(i + 1) * 512].rearrange("p (j c) -> p j c", j=4) for i in range(2)]
            ta = [arD[:, i * 512:(i + 1) * 512].rearrange("p (j c) -> p j c", j=4) for i in range(2)]
            pooled = [sb([128, 8, 128], BF16, "pooled%d" % i, pa) for i in range(2)]
            Ug = sb([128, 32, 128], BF16, "Ug", pa)
            rr_all = [[sb([128, 4, 128], F32, "rr%d_%d" % (k_, i), pa) for i in range(2)] for k_ in range(1)] * 2
            Hs_all = [[sb([128, 4, 129], F32, "Hs%d_%d" % (k_, i), pa) for i in range(2)] for k_ in range(1)] * 2
            rr = rr_all[0]
            Hs = Hs_all[0]
            th = [sb([128, 512], BF16, "th%d" % i, pa) for i in range(2)]
            fix16 = sb([128, 16], F32, "fix16", pa)
            tp_ = [sb([128, 4, 128], F32, "tp%d" % i, pa) for i in range(2)]

            UBK = ["ub0", "ub1", "ub2", "ub3"]
            XK = ["xnTs%d" % s_ for s_ in range(8)]
            op("dve", lambda e: e.memset(ub[:, :, 0:16], 0.0), writes=UBK)

            def s5_core(nch, Ug_ap, hs_src, ncolY, yg_out_fn, ugk, hsk):
                for gb in range(8):
                    bank = pb[gb % 2]
                    bk = "pb%d" % (gb % 2)
                    for gq in range(4):
                        g = gb * 4 + gq
                        J, gl = g // 2, g % 2
                        rows = slice(gl * 64, gl * 64 + 64)
                        o = bank[0:nch, gq * ncolY:(gq + 1) * ncolY]
                        op("pe", lambda e, o=o, g=g: e.matmul(o, Ug_ap(g), toep_sb[:, g, 0:ncolY], start=True, stop=False),
                           reads=[ugk, "toep_sb"], writes=[bk], sig=False)
                        op("pe", lambda e, o=o, J=J, gl=gl, rows=rows: e.matmul(o, hs_src(0, J, gl), cpow_sb[rows, J, 0, 0:ncolY],
                                                                                start=False, stop=False),
                           reads=[hsk, "cpow_sb"], writes=[bk], sig=False)
                        op("pe", lambda e, o=o, J=J, gl=gl, rows=rows: e.matmul(o, hs_src(1, J, gl), cpow_sb[rows, J, 1, 0:ncolY],
                                                                                start=False, stop=True),
                           reads=[hsk, "cpow_sb"], writes=[bk], sig=(gq == 3))
                    yg_out_fn(gb, bank, bk)

            for S in range(2):
                def ab1(s_):
                    i = s_ % 3
                    src = xp[1024 * S:1024 * (S + 1), :].rearrange("(c s) d -> s c d", s=8)[s_]
                    dma("sp", xa[i][:], src, writes=["xa%d" % i], sem="xa%d" % i)
                    return rstd_from([(xa[i][:], "xa%d" % i)], 128, None)

                def ab2(s_, r_ap, r_k):
                    i = s_ % 3
                    op("dve", lambda e: e.scalar_tensor_tensor(out=xn[i][:], in0=xa[i][:], scalar=r_ap, in1=g0[:],
                                                               op0=ALU.mult, op1=ALU.mult),
                       reads=["xa%d" % i, r_k, "g0"], writes=["xn%d" % i])
                    tb = ptb[s_ % 2]
                    tk = "pb%d" % (6 + s_ % 2)
                    for kt in range(8):
                        op("pe", lambda e, kt=kt: e.transpose(tb[:, kt * 128:(kt + 1) * 128], xn[i][:, kt * 128:(kt + 1) * 128], identb[:]),
                           reads=["xn%d" % i, "identb"], writes=[tk], sig=(kt == 7))
                    op("act", lambda e: e.activation(out=xnT[:, :, s_, :], in_=tb[:].rearrange("p (k c) -> p k c", c=128), func=AF.Copy),
                       reads=[tk], writes=["xnTs%d" % s_, "arA"])

                def ab3(s_):
                    bank = pb[s_ % 2]
                    bk = "pb%d" % (s_ % 2)
                    for kt in range(8):
                        op("pe", lambda e, kt=kt: e.matmul(bank[:], xnT[:, kt, s_, :], w_in_sb[:, kt, 0:512], start=(kt == 0), stop=(kt == 7)),
                           reads=["xnTs%d" % s_, "w_in_sb"], writes=[bk], sig=(kt == 7))
                    op("dve", lambda e: e.tensor_copy(U_sb[:, :, s_, :], bank[:].rearrange("p (g h) -> p g h", h=16)),
                       reads=[bk], writes=["arB"])

                rq = {0: ab1(0), 1: ab1(1), 2: ab1(2)}
                ab2(0, *rq[0])
                ab2(1, *rq[1])
                def poolhalf(nh):
                    for gi in range(4):
                        bank = pb[2 + gi % 2]
                        bk = "pb%d" % (2 + gi % 2)
                        for kt in range(8):
                            op("pe", lambda e, gi=gi, kt=kt, bank=bank: e.matmul(
                                bank[:], w_in_sb[:, kt, 512 + gi * 128:512 + (gi + 1) * 128],
                                xnT[:, kt, nh * 4:(nh + 1) * 4, :].rearrange("p s c -> p (s c)"), start=(kt == 0), stop=(kt == 7)),
                               reads=["xnTs%d" % s_ for s_ in range(nh * 4, nh * 4 + 4)] + ["w_in_sb"], writes=[bk], sig=(kt == 7))
                        dst = ub[:, gi, 16:1040].rearrange("p (c s) -> p s c", s=8)[:, nh * 4:(nh + 1) * 4, :]
                        if gi % 2 == 0:
                            op("act", lambda e, dst=dst, bank=bank: e.activation(out=dst, in_=bank[:].rearrange("p (s c) -> p s c", c=128), func=AF.Copy),
                               reads=[bk], writes=["ub%d" % gi])
                        else:
                            op("dve", lambda e, dst=dst, bank=bank: e.tensor_copy(dst, bank[:].rearrange("p (s c) -> p s c", c=128)),
                               reads=[bk], writes=["ub%d" % gi])

                for s in range(8):
                    if s + 3 < 8:
                        rq[s + 3] = ab1(s + 3)
                    if s + 2 < 8:
                        ab2(s + 2, *rq[s + 2])
                    ab3(s)
                    if s == 4:
                        poolhalf(0)
                poolhalf(1)
                dcur = {}

                def d1(gi):
                    w = POOL_W[gi]
                    a_ = ub[:, gi, :]
                    lv = [(wa, "wa", 1), (wb, "wb", 2), (wa, "wa", 4), (wb, "wb", 8)]
                    nlev = int(math.log2(w))
                    cur, curk = a_, "ub%d" % gi
                    for li in range(nlev):
                        o_, ok, sh = lv[li]
                        lo = 2 * sh - 1
                        src_ = cur
                        op("dve", lambda e, o_=o_, src_=src_, lo=lo, sh=sh: e.tensor_add(o_[:, lo:1040], src_[:, lo:1040], src_[:, lo - sh:1040 - sh]),
                           reads=[curk], writes=[ok])
                        cur, curk = o_, ok
                    dcur[gi] = (cur, curk)

                def d2(gi):
                    w = POOL_W[gi]
                    a_ = ub[:, gi, :]
                    cur, curk = dcur[gi]
                    pl = pooled[gi % 2]
                    pk = "pooled%d" % (gi % 2)
                    op("dve", lambda e: e.scalar_tensor_tensor(
                        out=pl[:], in0=cur[:, 16:1040].rearrange("p (c s) -> p s c", s=8), scalar=1.0 / w,
                        in1=a_[:, 16:1040].rearrange("p (c s) -> p s c", s=8), op0=ALU.mult, op1=ALU.subtract),
                       reads=[curk, "ub%d" % gi], writes=[pk])
                    if S == 0:
                        op("dve", lambda e: e.tensor_mul(fix16[:], cur[:, 16:32], invcnt[:, gi, :]),
                           reads=[curk, "invcnt"], writes=["fix16"])
                        op("dve", lambda e: e.tensor_sub(pl[:, :, 0:2], fix16[:].rearrange("p (c s) -> p s c", s=8),
                                                         a_[:, 16:32].rearrange("p (c s) -> p s c", s=8)),
                           reads=["fix16", "ub%d" % gi], writes=[pk])
                    for nh in range(2):
                        bank = pb[4 + nh]
                        bk = "pb%d" % (4 + nh)
                        op("pe", lambda e, nh=nh, bank=bank: e.matmul(
                            bank[:], poolw_sb[:, gi, :], pl[:, nh * 4:(nh + 1) * 4, :].rearrange("p s c -> p (s c)"), start=True, stop=True),
                           reads=[pk, "poolw_sb"], writes=[bk])
                        op("act", lambda e, nh=nh, bank=bank: e.activation(
                            out=ycat[:, 4 + gi, 1024 * S + 512 * nh:1024 * S + 512 * (nh + 1)], in_=bank[:], func=AF.Copy, scale=pscale[:, gi:gi + 1]),
                           reads=[bk, "pscale"], writes=["ycat"])

                for gb in range(4):
                    tb = ptb[gb % 2]
                    tk = "pb%d" % (6 + gb % 2)
                    for gq in range(8):
                        g = gb * 8 + gq
                        op("pe", lambda e, g=g, gq=gq, tb=tb: e.transpose(tb[:, gq * 128:(gq + 1) * 128],
                                                                           U_sb[:, g, :, :].rearrange("p s h -> p (s h)"), identb[:]),
                           reads=["arB", "identb"], writes=[tk], sig=(gq == 7))
                    op("act", lambda e, gb=gb, tb=tb: e.activation(out=Ug[:, gb * 8:(gb + 1) * 8, :], in_=tb[:].rearrange("p (g c) -> p g c", c=128),
                                                                 func=AF.Copy), reads=[tk], writes=["Ug"])
                for pbt in range(4):
                    rr = rr_all[pbt % 2]
                    Hs = Hs_all[pbt % 2]
                    rrk = ["rr0_%d" % i for i in range(2)]
                    hsk = ["Hs0_%d" % i for i in range(2)]
                    for part in range(2):
                        bank = pb[4 + part]
                        bk = "pb%d" % (4 + part)
                        for j in range(4):
                            J = pbt * 4 + j
                            for gl in range(2):
                                g = 2 * J + gl
                                op("pe", lambda e, bank=bank, j=j, gl=gl, g=g, part=part: e.matmul(
                                    bank[gl * 64:(gl + 1) * 64, j * 128:(j + 1) * 128], bpow_sb[:, g, part, :], Ug[:, g, :], start=True, stop=True),
                                   reads=["bpow_sb", "Ug"], writes=[bk], sig=(j == 3 and gl == 1))
                    Js = slice(pbt * 4, pbt * 4 + 4)
                    Gr = pb[4][:].rearrange("p (j c) -> p j c", c=128)
                    Gi = pb[5][:].rearrange("p (j c) -> p j c", c=128)
                    tt = lambda o_, ok, a_, ak, b_, bkk, fn="tensor_mul": op(
                        "dve", lambda e: getattr(e, fn)(o_, a_, b_), reads=[ak, bkk], writes=[ok])
                    tt(ta[0][:], "arD", ER[:, Js, :], "ER", Gr, "pb4")
                    tt(ta[1][:], "arD", EI[:, Js, :], "EI", Gi, "pb5")
                    tt(Gm[0][:], "arC", ta[0][:], "arD", ta[1][:], "arD", "tensor_sub")
                    tt(ta[0][:], "arD", ER[:, Js, :], "ER", Gi, "pb5")
                    tt(ta[1][:], "arD", EI[:, Js, :], "EI", Gr, "pb4")
                    tt(Gm[1][:], "arC", ta[0][:], "arD", ta[1][:], "arD", "tensor_add")
                    for part in range(2):
                        for j in range(4):
                            J = pbt * 4 + j
                            op("dve", lambda e, part=part, j=j, J=J: e.tensor_tensor_scan(
                                out=rr[part][:, j, :], data0=rho8[:, J:J + 1].to_broadcast([128, 128]), data1=Gm[part][:, j, :],
                                initial=carry[:, part, J:J + 1], op0=ALU.mult, op1=ALU.add),
                               reads=["arC", "rho8", "carry%d" % pbt], writes=[rrk[part]], sig=(j == 3))
                    for part in range(2):
                        op("dve", lambda e, part=part, Js=Js: e.tensor_copy(Hs[part][:, :, 0], carry[:, part, Js]),
                           reads=["carry%d" % pbt], writes=[hsk[part]])
                    tp2 = lambda eng_, o_, ok, a_, ak, b_, bkk, fn="tensor_mul": op(
                        eng_, lambda e: getattr(e, fn)(o_, a_, b_), reads=[ak, bkk], writes=[ok])
                    tp2("dve", tp_[0][:], "tp0", ER[:, Js, :], "ER", rr[0][:], rrk[0])
                    tp2("dve", tp_[1][:], "tp1", EI[:, Js, :], "EI", rr[1][:], rrk[1])
                    tp2("dve", Hs[0][:, :, 1:129], hsk[0], tp_[0][:], "tp0", tp_[1][:], "tp1", "tensor_add")
                    tp2("dve", ta[0][:], "arD", ER[:, Js, :], "ER", rr[1][:], rrk[1])
                    tp2("dve", ta[1][:], "arD", EI[:, Js, :], "EI", rr[0][:], rrk[0])
                    tp2("dve", Hs[1][:, :, 1:129], hsk[1], ta[0][:], "arD", ta[1][:], "arD", "tensor_sub")
                    for part in range(2):
                        op("dve", lambda e, part=part, Js=Js: e.tensor_copy(carry[:, part, Js], Hs[part][:, :, 128]),
                           reads=[hsk[part]], writes=["carry%d" % pbt])
                        op("act", lambda e, part=part, Js=Js: e.activation(out=Hsb[:, part, Js, :], in_=Hs[part][:, :, 0:128], func=AF.Copy),
                           reads=[hsk[part]], writes=["arB"])

                def yg_out(gb, bank, bk):
                    dst = ygU[:].rearrange("p t (g h) -> p g t h", h=16)[:, gb * 4:(gb + 1) * 4, :, :]
                    op("act", lambda e: e.activation(out=dst, in_=bank[:].rearrange("p (g t h) -> p g t h", t=8, h=16), func=AF.Gelu_apprx_tanh),
                       reads=[bk], writes=["arA"] + XK)

                d1(0)
                s5_core(128, lambda g: Ug[:, g, :], lambda part, J, gl: Hsb[gl * 64:(gl + 1) * 64, part, J, :], 128, yg_out, "Ug", "arB")
                d2(0)
                d1(1)
                for tp in range(4):
                    tb = ptb[tp % 2]
                    tk = "pb%d" % (6 + tp % 2)
                    for t2 in range(2):
                        t = tp * 2 + t2
                        for ct in range(4):
                            op("pe", lambda e, t=t, t2=t2, ct=ct, tb=tb: e.transpose(tb[:, (t2 * 4 + ct) * 128:(t2 * 4 + ct + 1) * 128],
                                                                                  ygU[:, t, ct * 128:(ct + 1) * 128], identb[:]),
                               reads=["arA", "identb"], writes=[tk], sig=(t2 == 1 and ct == 3))
                    op("act", lambda e, tp=tp, tb=tb: e.activation(out=ygfm[:, :, tp * 2:tp * 2 + 2, :].rearrange("p ct t c -> p t ct c"),
                                                                 in_=tb[:].rearrange("p (t ct c) -> p t ct c", ct=4, c=128), func=AF.Copy),
                       reads=[tk], writes=["arA"] + XK)
                d2(1)
                d1(2)
                for ct in range(4):
                    for nh in range(2):
                        bank = pb[2 + nh]
                        bk = "pb%d" % (2 + nh)
                        gsrc = ygfm[:, ct, nh * 4:(nh + 1) * 4, :].rearrange("p t c -> p (t c)")
                        op("pe", lambda e, ct=ct, bank=bank, gsrc=gsrc: e.matmul(bank[:], glu_sb[:, ct, :], gsrc, start=True, stop=True),
                           reads=["arA", "glu_sb"], writes=[bk])
                        op("act", lambda e, nh=nh, bank=bank: e.activation(out=th[nh][:], in_=bank[:], func=AF.Tanh, scale=0.5),
                           reads=[bk], writes=["th%d" % nh])
                        op("dve", lambda e, ct=ct, nh=nh, gsrc=gsrc, S=S: e.scalar_tensor_tensor(
                            out=ycat[:, ct, 1024 * S + 512 * nh:1024 * S + 512 * (nh + 1)], in0=th[nh][:], scalar=1.0, in1=gsrc,
                            op0=ALU.add, op1=ALU.mult), reads=["th%d" % nh, "arA"], writes=["ycat"])
                d2(2)
                d1(3)
                d2(3)
                if S == 1:
                    for gi in range(4):
                        dma("sp", o_pool_p[:, gi * 128:(gi + 1) * 128].rearrange("t c -> c t"), ub[:, gi, 1025:1040], reads=["ub%d" % gi], sem="out", is_out=True)
                op("dve", lambda e: e.tensor_copy(ub[:, :, 0:16], ub[:, :, 1024:1040]), reads=UBK, writes=UBK)
                if stop == "S%d" % S:
                    kb.halted = True
            dma("sp", o_re_p.rearrange("(j q) -> q j", q=128), carry[:, 0, :], reads=["carry0", "carry1", "carry2", "carry3"], sem="out", is_out=True)
            dma("sp", o_im_p.rearrange("(j q) -> q j", q=128), carry[:, 1, :], reads=["carry0", "carry1", "carry2", "carry3"], sem="out", is_out=True)

            R = NSMP
            if stop == "p1":
                kb.halted = True
            barrier()
            pa.close()
            with ExitStack() as sm:
                xas = sb([R, D], F32, "xas", sm)
                xns = sb([R, D], BF16, "xns", sm)
                xnTs = sb([128, 8, R], BF16, "xnTs", sm)
                tA = sb([128, 16, R], F32, "tA", sm)
                tB = sb([128, 16, R], F32, "tB", sm)
                u_s = sb([R, D], F32, "u_s", sm)
                stt_ = [sb([R, 2048], F32, "stt%d" % i, sm) for i in range(2)]
                h0 = [sb([128, 16, R], F32, "h0_%d" % i, sm) for i in range(2)]
                h0b = sb([128, 2, 16, R], BF16, "h0b", sm)
                hn_ = [sb([128, 16, R], F32, "hn_%d" % i, sm) for i in range(2)]
                Uz = sb([R, 32, 8, 16], BF16, "Uz", sm)
                Ugs = sb([128, 32, R], BF16, "Ugs", sm)
                ygs = sb([R, 512], BF16, "ygs", sm)
                ygfs = sb([128, 4, R], BF16, "ygfs", sm)
                ths = sb([128, R], BF16, "ths", sm)
                Ep = sb([R, 26, 128], F32, "Ep", sm)
                red = sb([R, 512], F32, "red", sm)
                pls = sb([R, 512], BF16, "pls", sm)
                plT = sb([128, 4, R], BF16, "plT", sm)
                sto = sb([R, 2048], F32, "sto", sm)

                dma("sp", xas[:], xs, writes=["xas"], sem="smp")
                dma("sp", stt_[0][:], st_re, writes=["stt0"], sem="smp")
                dma("sp", stt_[1][:], st_im, writes=["stt1"], sem="smp")
                roff = 0
                for gi, w in enumerate(POOL_W):
                    dma("sp", Ep[:, roff:roff + w - 1, :], st_pool[:, 15 - (w - 1):15, gi * 128:(gi + 1) * 128], writes=["Ep"], sem="smp")
                    roff += w - 1
                dma("sp", o_pool_s[:, 0:14, :], st_pool[:, 1:15, :], sem="out", is_out=True)
                r_ap, r_k = rstd_from([(xas[:], "xas")], R, None)
                op("dve", lambda e: e.scalar_tensor_tensor(out=xns[:], in0=xas[:], scalar=r_ap, in1=g0[0:R, :],
                                                           op0=ALU.mult, op1=ALU.mult), reads=["xas", r_k, "g0"], writes=["xns"])
                for kt in range(8):
                    op("pe", lambda e, kt=kt: e.transpose(ptb[0][:, kt * R:(kt + 1) * R], xns[:, kt * 128:(kt + 1) * 128], identb[0:R, 0:R]),
                       reads=["xns", "identb"], writes=["pb6"], sig=(kt == 7))
                op("act", lambda e: e.activation(out=xnTs[:], in_=ptb[0][:, 0:8 * R].rearrange("p (k c) -> p k c", c=R), func=AF.Copy),
                   reads=["pb6"], writes=["xnTs"])
                for half in range(2):
                    for kt in range(8):
                        op("pe", lambda e, half=half, kt=kt: e.matmul(pb[half][0:R, :], xnTs[:, kt, :], w_in_sb[:, kt, half * 512:(half + 1) * 512],
                                                                      start=(kt == 0), stop=(kt == 7)),
                           reads=["xnTs", "w_in_sb"], writes=["pb%d" % half], sig=(kt == 7))
                    op("dve", lambda e, half=half: e.tensor_copy(u_s[:, half * 512:(half + 1) * 512], pb[half][0:R, :]),
                       reads=["pb%d" % half], writes=["u_s"])
                dma("sp", o_pool_s[:, 14, :], u_s[:, 512:1024], reads=["u_s"], sem="out", is_out=True)
                roff = 0
                for gi, w in enumerate(POOL_W):
                    cols = slice(gi * 128, (gi + 1) * 128)
                    if w - 1 > 1:
                        op("dve", lambda e, roff=roff, w=w, cols=cols: e.tensor_reduce(
                            out=red[:, cols], in_=Ep[:, roff:roff + w - 1, :].rearrange("p r c -> p c r"), axis=AX.X, op=ALU.add),
                           reads=["Ep"], writes=["red"])
                    else:
                        op("dve", lambda e, roff=roff, cols=cols: e.tensor_copy(red[:, cols], Ep[:, roff, :]), reads=["Ep"], writes=["red"])
                    roff += w - 1
                    uc = u_s[:, 512 + gi * 128:512 + (gi + 1) * 128]
                    op("dve", lambda e, cols=cols, uc=uc: e.tensor_add(red[:, cols], red[:, cols], uc), reads=["red", "u_s"], writes=["red"])
                    op("dve", lambda e, cols=cols, uc=uc, w=w: e.scalar_tensor_tensor(out=pls[:, cols], in0=red[:, cols], scalar=1.0 / w, in1=uc,
                                                                                      op0=ALU.mult, op1=ALU.subtract),
                       reads=["red", "u_s"], writes=["pls"])
                for gi in range(4):
                    op("pe", lambda e, gi=gi: e.transpose(ptb[1][:, gi * R:(gi + 1) * R], pls[:, gi * 128:(gi + 1) * 128], identb[0:R, 0:R]),
                       reads=["pls", "identb"], writes=["pb7"], sig=(gi == 3))
                op("act", lambda e: e.activation(out=plT[:], in_=ptb[1][:, 0:4 * R].rearrange("p (g c) -> p g c", c=R), func=AF.Copy),
                   reads=["pb7"], writes=["plT"])
                for gi in range(4):
                    op("pe", lambda e, gi=gi: e.matmul(pb[2][:, gi * R:(gi + 1) * R], poolw_sb[:, gi, :], plT[:, gi, :], start=True, stop=True),
                       reads=["plT", "poolw_sb"], writes=["pb2"], sig=(gi == 3))
                for gi in range(4):
                    op("act", lambda e, gi=gi: e.activation(out=ycat[:, 4 + gi, SEQ:SEQ + R], in_=pb[2][:, gi * R:(gi + 1) * R], func=AF.Copy,
                                                            scale=pscale[:, gi:gi + 1]), reads=["pb2", "pscale"], writes=["ycat"])
                for part in range(2):
                    for J in range(16):
                        op("pe", lambda e, part=part, J=J: e.matmul(pb[3][:, (part * 16 + J) * R:(part * 16 + J + 1) * R],
                                                                    stt_[part][:, J * 128:(J + 1) * 128], ident32[0:R, 0:R], start=True, stop=True),
                           reads=["stt%d" % part, "ident32"], writes=["pb3"], sig=(J == 15))
                    op("dve", lambda e, part=part: e.tensor_copy(h0[part][:], pb[3][:, part * 16 * R:(part + 1) * 16 * R].rearrange("p (j b) -> p j b", b=R)),
                       reads=["pb3"], writes=["h0_%d" % part])
                    op("act", lambda e, part=part: e.activation(out=h0b[:, part, :, :], in_=h0[part][:], func=AF.Copy),
                       reads=["h0_%d" % part], writes=["h0b"])
                op("dve", lambda e: e.memset(Uz[:], 0.0), writes=["Uz"])
                for s in (0, 7):
                    op("dve", lambda e, s=s: e.tensor_copy(Uz[:, :, s, :], u_s[:, 0:512].rearrange("p (g h) -> p g h", h=16)),
                       reads=["u_s"], writes=["Uz"])
                for gb in range(4):
                    tb = ptb[gb % 2]
                    tk = "pb%d" % (6 + gb % 2)
                    for gq in range(8):
                        g = gb * 8 + gq
                        op("pe", lambda e, g=g, gq=gq, tb=tb: e.transpose(tb[:, gq * R:(gq + 1) * R], Uz[:, g, :, :].rearrange("p s h -> p (s h)"),
                                                                           identb[0:R, 0:R]), reads=["Uz", "identb"], writes=[tk], sig=(gq == 7))
                    op("act", lambda e, gb=gb, tb=tb: e.activation(out=Ugs[:, gb * 8:(gb + 1) * 8, :], in_=tb[:, 0:8 * R].rearrange("p (g c) -> p g c", c=R),
                                                                 func=AF.Copy), reads=[tk], writes=["Ugs"])
                for part in range(2):
                    bank = pb[4 + part]
                    bk = "pb%d" % (4 + part)
                    for g in range(32):
                        J, gl = g // 2, g % 2
                        op("pe", lambda e, bank=bank, J=J, gl=gl, g=g, part=part: e.matmul(
                            bank[gl * 64:(gl + 1) * 64, J * R:(J + 1) * R], bpow_sb[64:128, g, part, :], Ugs[64:128, g, :], start=True, stop=True),
                           reads=["bpow_sb", "Ugs"], writes=[bk], sig=(g == 31))
                b3 = lambda t_: t_[:].unsqueeze(2).to_broadcast([128, 16, R])
                Gsr = pb[4][:, 0:16 * R].rearrange("p (j b) -> p j b", b=R)
                Gsi = pb[5][:, 0:16 * R].rearrange("p (j b) -> p j b", b=R)
                t0 = tA[:]
                t1 = tB[:]
                op("dve", lambda e: e.tensor_mul(t0, h0[0][:], b3(P1r)), reads=["h0_0", "P1r"], writes=["tAB"])
                op("dve", lambda e: e.tensor_mul(t1, h0[1][:], b3(P1i)), reads=["h0_1", "P1i"], writes=["tAB"])
                op("dve", lambda e: e.tensor_sub(t0, t0, t1), reads=["tAB"], writes=["tAB"])
                op("dve", lambda e: e.tensor_add(hn_[0][:], t0, Gsr), reads=["tAB", "pb4"], writes=["hn_0"])
                op("dve", lambda e: e.tensor_mul(t0, h0[0][:], b3(P1i)), reads=["h0_0", "P1i"], writes=["tAB"])
                op("dve", lambda e: e.tensor_mul(t1, h0[1][:], b3(P1r)), reads=["h0_1", "P1r"], writes=["tAB"])
                op("dve", lambda e: e.tensor_add(t0, t0, t1), reads=["tAB"], writes=["tAB"])
                op("dve", lambda e: e.tensor_add(hn_[1][:], t0, Gsi), reads=["tAB", "pb5"], writes=["hn_1"])
                for part, o_ap in ((0, o_re_s), (1, o_im_s)):
                    for J in range(16):
                        bank = pb[J // 4]
                        bk = "pb%d" % (J // 4)
                        op("pe", lambda e, part=part, J=J, bank=bank: e.matmul(bank[0:R, (J % 4) * 128:(J % 4 + 1) * 128], hn_[part][:, J, :], ident32[:],
                                                                               start=True, stop=True),
                           reads=["hn_%d" % part, "ident32"], writes=[bk], sig=(J % 4 == 3))
                        if J % 4 == 3:
                            op("dve", lambda e, J=J, bank=bank: e.tensor_copy(sto[:, (J // 4) * 512:(J // 4 + 1) * 512], bank[0:R, :]),
                               reads=[bk], writes=["sto"])
                    dma("sp", o_ap, sto[:], reads=["sto"], sem="out", is_out=True)

                def yg_out_s(gb, bank, bk):
                    op("act", lambda e: e.activation(out=ygs[:, gb * 64:(gb + 1) * 64], in_=bank[0:R, 0:64], func=AF.Gelu_apprx_tanh),
                       reads=[bk], writes=["ygs"])

                s5_core(R, lambda g: Ugs[:, g, :], lambda part, J, gl: h0b[gl * 64:(gl + 1) * 64, part, J, :], 16, yg_out_s, "Ugs", "h0b")
                for ct in range(4):
                    op("pe", lambda e, ct=ct: e.transpose(ptb[0][:, ct * R:(ct + 1) * R], ygs[:, ct * 128:(ct + 1) * 128], identb[0:R, 0:R]),
                       reads=["ygs", "identb"], writes=["pb6"], sig=(ct == 3))
                op("act", lambda e: e.activation(out=ygfs[:], in_=ptb[0][:, 0:4 * R].rearrange("p (g c) -> p g c", c=R), func=AF.Copy),
                   reads=["pb6"], writes=["ygfs"])
                for ct in range(4):
                    op("pe", lambda e, ct=ct: e.matmul(pb[2][:, ct * R:(ct + 1) * R], glu_sb[:, ct, :], ygfs[:, ct, :], start=True, stop=True),
                       reads=["ygfs", "glu_sb"], writes=["pb2"])
                    op("act", lambda e, ct=ct: e.activation(out=ths[:], in_=pb[2][:, ct * R:(ct + 1) * R], func=AF.Tanh, scale=0.5),
                       reads=["pb2"], writes=["ths"])
                    op("dve", lambda e, ct=ct: e.scalar_tensor_tensor(out=ycat[:, ct, SEQ:SEQ + R], in0=ths[:], scalar=1.0, in1=ygfs[:, ct, :],
                                                                      op0=ALU.add, op1=ALU.mult), reads=["ths", "ygfs"], writes=["ycat"])
                if dbg and "ycat" in dbg:
                    dma("sp", dbg["ycat"], ycat[:], reads=["ycat"], sem="out", is_out=True)
                barrier()

        p1.close()
        if stop == "sample":
            kb.halted = True
        w_out_sb = sb([128, 8, D], BF16, "w_out_sb")
        w_up_sb = sb([128, 8, DFF], BF16, "w_up_sb")
        w_dn_sb = sb([128, 32, D], BF16, "w_dn_sb")
        gpost = sb([128, 2, D], F32, "gpost")
        g2col = sb([128, 8], F32, "g2col")
        dma("sp", gpost[:, 0, :], g_mix_post.partition_broadcast(128), writes=["gpost"])
        dma("sp", gpost[:, 1, :], g_mlp_post.partition_broadcast(128), writes=["gpost"])
        dma("sp", g2col[:], g_mlp_pre.rearrange("(k p) -> p k", p=128), writes=["g2col"])
        wov = w_out.rearrange("(kt p) n -> p kt n", p=128)
        for h_ in range(2):
            dma("pool", w_out_sb[:, 4 * h_:4 * h_ + 4, :], wov[:, 4 * h_:4 * h_ + 4, :], writes=["w_out_sb"])
        wuv = w_up.rearrange("(kt p) n -> p kt n", p=128)
        wdv = w_dn.rearrange("(ft p) n -> p ft n", p=128)
        for fc in range(8):
            dma("pool", w_up_sb[:, :, fc * 512:(fc + 1) * 512], wuv[:, :, fc * 512:(fc + 1) * 512], writes=["w_up%d" % fc])
            dma("pool", w_dn_sb[:, fc * 4:(fc + 1) * 4, :], wdv[:, fc * 4:(fc + 1) * 4, :], writes=["w_dn%d" % fc])
        op("dve", lambda e: e.tensor_scalar_mul(out=w_out_sb[:, 0:4, :], in0=w_out_sb[:, 0:4, :], scalar1=0.5),
           reads=["w_out_sb"], writes=["w_out_sb"])
        def w_scale(fc):
            for kt in range(8):
                op("dve", lambda e, kt=kt: e.tensor_scalar_mul(out=w_up_sb[:, kt, fc * 512:(fc + 1) * 512],
                                                               in0=w_up_sb[:, kt, fc * 512:(fc + 1) * 512], scalar1=g2col[:, kt:kt + 1]),
                   reads=["w_up%d" % fc, "g2col"], writes=["w_up%d" % fc])

        xr = [sb([128, D], F32, "xr%d" % i) for i in range(3)]
        hn = [sb([128, D], BF16, "hn%d" % i) for i in range(2)]
        hnT = sb([128, 8, 256], BF16, "hnT")
        ffT = [sb([128, 256], BF16, "ffT%d" % i) for i in range(4)]
        acc = pb[0:4]
        upb = [pb[4], pb[5]]
        v3 = lambda ap_: ap_.rearrange("p (a b) -> p a b", b=128)

        units = []
        for S in range(2):
            for s2 in range(4):
                units.append([(128, 1024 * S + 128 * (2 * s2 + i),
                               xp[1024 * S:1024 * (S + 1), :].rearrange("(c s) d -> s c d", s=8)[2 * s2 + i],
                               y_p[1024 * S:1024 * (S + 1), :].rearrange("(c s) d -> s c d", s=8)[2 * s2 + i]) for i in range(2)])
        units.append([(NSMP, SEQ, xs, y_s)])
        NU = len(units)

        free_slots = [("xr%d" % i, v3(xr[i][:])) for i in range(3)]
        slot_of = {}

        def ycat_slot(u):
            c0 = units[u][0][1]
            return ("yc%d" % u, ycat[:, :, c0:c0 + 256].bitcast(F32))

        def pre(u, i):
            rows, col0, xsrc, ydst = units[u][i]
            hk, H = free_slots.pop(0)
            slot_of[(u, i)] = (hk, H)
            Hh = [H[0:rows, 0:4, :], H[0:rows, 4:8, :]]
            m = [pb[6], pb[7]]
            mk = ["pb6", "pb7"]
            dma("sp", H[0:rows], xsrc.rearrange("r (a b) -> r a b", b=128), writes=[hk])
            for half in range(2):
                for ct in range(8):
                    op("pe", lambda e, half=half, ct=ct: e.matmul(
                        m[half][0:rows, :], ycat[:, ct, col0:col0 + rows], w_out_sb[:, ct, half * 512:(half + 1) * 512],
                        start=(ct == 0), stop=(ct == 7)), reads=["yc%d" % u, "w_out_sb"], writes=[mk[half]], sig=(ct == 7))
            r_ap, r_k = rstd_from([(m[h_][0:rows, :], mk[h_]) for h_ in range(2)], rows, None)
            for half in range(2):
                cs_ = slice(half * 512, (half + 1) * 512)
                op("dve", lambda e, half=half, cs_=cs_: e.scalar_tensor_tensor(
                    out=m[half][0:rows, :], in0=m[half][0:rows, :], scalar=r_ap, in1=gpost[0:rows, 0, cs_],
                    op0=ALU.mult, op1=ALU.mult), reads=[mk[half], r_k, "gpost"], writes=[mk[half]])
                op("dve", lambda e, half=half: e.tensor_add(Hh[half], v3(m[half][0:rows, :]), Hh[half]),
                   reads=[mk[half], hk], writes=[hk])
            cA, cB = stat_col(), stat_col()
            for half, c_ in ((0, cA), (1, cB)):
                op("dve", lambda e, half=half, c_=c_: e.scalar_tensor_tensor(
                    out=v3(hn[i][0:rows, half * 512:(half + 1) * 512]), in0=Hh[half], scalar=1.0 / 1024.0, in1=Hh[half],
                    op0=ALU.mult, op1=ALU.mult, accum_out=stat[0:rows, c_:c_ + 1]),
                   reads=[hk], writes=["hn%d" % i, "stat%d" % c_])
            cC = stat_col()
            op("dve", lambda e: e.scalar_tensor_tensor(out=stat[0:rows, cC:cC + 1], in0=stat[0:rows, cA:cA + 1], scalar=EPS,
                                                       in1=stat[0:rows, cB:cB + 1], op0=ALU.add, op1=ALU.add),
               reads=["stat%d" % cA, "stat%d" % cB], writes=["stat%d" % cC])
            cD = stat_col()
            op("pool", lambda e: e.tensor_tensor(out=stat[0:rows, cD:cD + 1], in0=stat[0:rows, cC:cC + 1], in1=cst[0:rows, 1:2], op=ALU.pow),
               reads=["stat%d" % cC, "cst"], writes=["stat%d" % cD])
            r2_ap, r2_k = stat[0:rows, cD:cD + 1], "stat%d" % cD
            op("dve", lambda e: e.tensor_scalar_mul(out=v3(hn[i][0:rows, :]), in0=H[0:rows], scalar1=r2_ap),
               reads=[hk, r2_k], writes=["hn%d" % i])

        def tr(u, i):
            rows = units[u][i][0]
            tb = ptb[i]
            tk = "pb%d" % (6 + i)
            for kt in range(8):
                op("pe", lambda e, kt=kt: e.transpose(tb[:, kt * rows:(kt + 1) * rows], hn[i][0:rows, kt * 128:(kt + 1) * 128],
                                                      identb[0:rows, 0:rows]),
                   reads=["hn%d" % i, "identb"], writes=[tk], sig=(kt == 7))
            op("dve", lambda e: e.tensor_copy(hnT[:, :, i * 128:i * 128 + rows], tb[:, 0:8 * rows].rearrange("p (k c) -> p k c", c=rows)),
               reads=[tk], writes=["hnT"])

        def post(u, i):
            rows, col0, xsrc, ydst = units[u][i]
            hk, H = slot_of[(u, i)]
            Hh = [H[0:rows, 0:4, :], H[0:rows, 4:8, :]]
            r_ap, r_k = rstd_from([(acc[2 * i + h_][0:rows, :], "pb%d" % (2 * i + h_)) for h_ in range(2)], rows, None)
            for half in range(2):
                cs_ = slice(half * 512, (half + 1) * 512)
                bk = "pb%d" % (2 * i + half)
                op("dve", lambda e, half=half, cs_=cs_: e.scalar_tensor_tensor(
                    out=acc[2 * i + half][0:rows, :], in0=acc[2 * i + half][0:rows, :], scalar=r_ap, in1=gpost[0:rows, 1, cs_],
                    op0=ALU.mult, op1=ALU.mult), reads=[bk, r_k, "gpost"], writes=[bk])
                op("dve", lambda e, half=half: e.tensor_add(Hh[half], v3(acc[2 * i + half][0:rows, :]), Hh[half]),
                   reads=[bk, hk], writes=[hk])
            dma("sp", ydst.rearrange("r (a b) -> r a b", b=128), H[0:rows], reads=[hk], is_out=True)
            free_slots.append((hk, H))

        def mlp(u, hooks, tail_hook=None):
            unit = units[u]
            ntile = len(unit)
            ncol = 128 * (ntile - 1) + unit[-1][0]

            def down(ft):
                for i, (rows, col0, xsrc, ydst) in enumerate(unit):
                    for half in range(2):
                        op("pe", lambda e, i=i, half=half, rows=rows: e.matmul(
                            acc[2 * i + half][0:rows, :], ffT[ft % 4][:, i * 128:i * 128 + rows], w_dn_sb[:, ft, half * 512:(half + 1) * 512],
                            start=(ft == 0), stop=(ft == 31)), reads=["ffT%d" % (ft % 4), "w_dn%d" % (ft // 4)],
                           writes=["pb%d" % (2 * i + half)], sig=(ft == 31 or (i == ntile - 1 and half == 1)))
            DL = 3
            for ft in range(32):
                if u == 0 and ft % 4 == 0:
                    w_scale(ft // 4)
                u_ap = upb[ft % 2][:, 0:ncol]
                uk = "pb%d" % (4 + ft % 2)
                for kt in range(8):
                    op("pe", lambda e, kt=kt: e.matmul(u_ap, w_up_sb[:, kt, ft * 128:(ft + 1) * 128], hnT[:, kt, 0:ncol],
                                                      start=(kt == 0), stop=(kt == 7)),
                       reads=["w_up%d" % (ft // 4), "hnT"], writes=[uk], sig=(kt == 7))
                op("act", lambda e: e.activation(out=u_ap, in_=u_ap, func=AF.Relu), reads=[uk], writes=[uk])
                op("act", lambda e: e.activation(out=ffT[ft % 4][:, 0:ncol], in_=u_ap, func=AF.Square), reads=[uk], writes=["ffT%d" % (ft % 4)])
                if ft >= DL:
                    down(ft - DL)
                if ft in hooks:
                    hooks[ft]()
            if tail_hook is not None:
                tail_hook()
            for ft in range(32 - DL, 32):
                down(ft)

        for i in range(len(units[0])):
            pre(0, i)
        for i in range(len(units[0])):
            tr(0, i)
        free_slots.append(ycat_slot(0))
        for u in range(NU):
            hooks = {}
            if u + 1 < NU:
                nt = len(units[u + 1])
                hooks[1] = (lambda u=u: pre(u + 1, 0))
                if nt > 1:
                    hooks[16] = (lambda u=u: pre(u + 1, 1))
            th_ = None
            if u + 1 < NU:
                th_ = (lambda u=u: [tr(u + 1, i) for i in range(len(units[u + 1]))])
            mlp(u, hooks, th_)
            if u + 1 < NU:
                if u + 1 < 8:
                    free_slots.append(ycat_slot(u + 1))
            for i in range(len(units[u])):
                post(u, i)
      except _Stop:
        pass
      kb.halted = False
      kb.finish()
    return nc


_IN_ORDER = ["x_prompt", "x_sample", "state_s5_re", "state_s5_im", "state_pool", "norm_mix_pre", "norm_mix_post",
             "norm_mlp_pre", "norm_mlp_post", "w_in", "s5_lambda_re", "s5_lambda_im", "s5_log_dt", "s5_b_re", "s5_b_im",
             "s5_c_re", "s5_c_im", "s5_d", "s5_w_glu", "pool_w", "pool_scale", "w_out", "w_mlp_up", "w_mlp_down"]


def make_in_maps(inp):
    f = lambda a: np.ascontiguousarray(np.asarray(a, dtype=np.float32))
    shared = {
        "g_mix_pre": f(inp["norm_mix_pre"]), "g_mix_post": f(inp["norm_mix_post"]),
        "g_mlp_pre": f(inp["norm_mlp_pre"]), "g_mlp_post": f(inp["norm_mlp_post"]),
        "w_in": f(inp["w_in"]), "lam_re": f(inp["s5_lambda_re"]).reshape(2048), "lam_im": f(inp["s5_lambda_im"]).reshape(2048),
        "log_dt": f(inp["s5_log_dt"]), "b_re": f(inp["s5_b_re"]).reshape(2048, 16), "b_im": f(inp["s5_b_im"]).reshape(2048, 16),
        "c_re": f(inp["s5_c_re"]).reshape(512, 64), "c_im": f(inp["s5_c_im"]).reshape(512, 64), "s5_d": f(inp["s5_d"]),
        "w_glu": f(inp["s5_w_glu"]), "pool_w": f(inp["pool_w"]), "pool_scale": f(inp["pool_scale"]),
        "w_out": f(inp["w_out"]), "w_up": f(inp["w_mlp_up"]), "w_dn": f(inp["w_mlp_down"]),
    }
    xp = f(inp["x_prompt"])
    xs = f(inp["x_sample"]).reshape(128, D)
    sre = f(inp["state_s5_re"]).reshape(128, 2048)
    sim = f(inp["state_s5_im"]).reshape(128, 2048)
    spl = f(inp["state_pool"])
    maps = []
    for c in range(NCORES):
        m = dict(shared)
        m["xp"] = xp[c]
        m["xs"] = xs[c * NSMP:(c + 1) * NSMP]
        m["st_re"] = sre[c * NSMP:(c + 1) * NSMP]
        m["st_im"] = sim[c * NSMP:(c + 1) * NSMP]
        m["st_pool"] = spl[c * NSMP:(c + 1) * NSMP]
        maps.append(m)
    return maps


def run(inp, debug=None, stop=None):
    nc = build_program(debug, stop)
    res = run_bass_kernel_spmd(nc, make_in_maps(inp), core_ids=list(range(NCORES)))
    return res.results


def kernel(**inputs):
    r = run(inputs)
    cat = lambda k: np.stack([np.asarray(r[c][k]) for c in range(NCORES)], axis=0)
    y_p = cat("y_p").reshape(8, SEQ, D).astype(np.float32)
    y_s = np.concatenate([np.asarray(r[c]["y_s"]) for c in range(NCORES)], axis=0).reshape(128, 1, D).astype(np.float32)
    re_p = cat("o_re_p").reshape(8, 32, 64).astype(np.float32)
    im_p = cat("o_im_p").reshape(8, 32, 64).astype(np.float32)
    pool_p = cat("o_pool_p").reshape(8, 15, 512).astype(np.float32)
    re_s = np.concatenate([np.asarray(r[c]["o_re_s"]) for c in range(NCORES)], axis=0).reshape(128, 32, 64).astype(np.float32)
    im_s = np.concatenate([np.asarray(r[c]["o_im_s"]) for c in range(NCORES)], axis=0).reshape(128, 32, 64).astype(np.float32)
    pool_s = np.concatenate([np.asarray(r[c]["o_pool_s"]) for c in range(NCORES)], axis=0).reshape(128, 15, 512).astype(np.float32)
    return (y_p, y_s, re_p, im_p, pool_p, re_s, im_s, pool_s)
```
